# Optimizing a Trainium2 kernel written in Bass

```python
import math
import jax, jax.numpy as jnp
from jax import lax
import numpy as np

D_MODEL = 1024
BATCH = 4
SEQ = 4096
DEPTH = 2

GRID_W = 64
CTX_LEN = 256
N_MIXERS = 4
GROUP_WIDTH = D_MODEL // N_MIXERS
HEAD_DIM = 64
A_HEADS = GROUP_WIDTH // HEAD_DIM
A_KV_HEADS = A_HEADS // 2
WINDOW = 128
BLK = 128
B_HEADS = GROUP_WIDTH // HEAD_DIM
B_QK_DIM = HEAD_DIM // 2
C_HEADS = GROUP_WIDTH // HEAD_DIM
RET_CHUNK = 128
D_HEADS = GROUP_WIDTH // HEAD_DIM
NA_ROWS = 8
NA_COLS = 16
NA_QCB = 16
NA_KCB = 2 * NA_QCB
D_FF = 4 * D_MODEL
ROPE_BASE = 10000.0
EPS = 1e-6
NEG_INF = -1e30
IN_WIDTHS = (A_HEADS * HEAD_DIM, A_KV_HEADS * HEAD_DIM, A_KV_HEADS * HEAD_DIM,
             2 * B_HEADS * B_QK_DIM, 2 * B_HEADS * B_QK_DIM, B_HEADS * HEAD_DIM,
             C_HEADS * HEAD_DIM, C_HEADS * HEAD_DIM, C_HEADS * HEAD_DIM, C_HEADS * HEAD_DIM,
             D_HEADS * HEAD_DIM, D_HEADS * HEAD_DIM, D_HEADS * HEAD_DIM)
IN_WIDTH = sum(IN_WIDTHS)

kernel_name = 'hybrid_parallel_heads_dit_block'


def rms_norm(x, g):
    xf = x.astype(jnp.float32)
    y = xf * lax.rsqrt(jnp.mean(xf * xf, axis=-1, keepdims=True) + EPS)
    return (y * g.astype(jnp.float32)).astype(x.dtype)


def softmax_f32(s):
    return jax.nn.softmax(s.astype(jnp.float32), axis=-1)


def to_heads(t, n, d):
    b, l, _ = t.shape
    return t.reshape(b, l, n, d).transpose(0, 2, 1, 3)


def from_heads(t):
    b, n, l, d = t.shape
    return t.transpose(0, 2, 1, 3).reshape(b, l, n * d)


def to_diff_heads(t):
    b, l, _ = t.shape
    return t.reshape(b, l, B_HEADS, 2, B_QK_DIM).transpose(0, 2, 3, 1, 4)


def split_in(t):
    bounds = np.cumsum(IN_WIDTHS)[:-1].tolist()
    return jnp.split(t, bounds, axis=-1)


def rope_1d(x, pos):
    half = x.shape[-1] // 2
    freqs = ROPE_BASE ** (-jnp.arange(half, dtype=jnp.float32) / half)
    ang = pos[:, None] * freqs[None, :]
    cos, sin = jnp.cos(ang), jnp.sin(ang)
    xf = x.astype(jnp.float32)
    x1, x2 = xf[..., :half], xf[..., half:]
    return jnp.concatenate([x1 * cos - x2 * sin, x1 * sin + x2 * cos], axis=-1).astype(x.dtype)


def rope_2d(x, rows, cols):
    h = x.shape[-1] // 2
    return jnp.concatenate([rope_1d(x[..., :h], rows), rope_1d(x[..., h:], cols)], axis=-1)


def window_attention(q, k, v, k_ctx, v_ctx, sink):
    b, ha, l, dh = q.shape
    g = k.shape[1]
    rep = ha // g
    nb = l // BLK
    scale = dh ** -0.5
    qb = q.reshape(b, g, rep, nb, BLK, dh)

    def band(t):
        tp = jnp.pad(t, ((0, 0), (0, 0), (BLK, BLK), (0, 0))).reshape(b, g, nb + 2, BLK, dh)
        return jnp.concatenate([tp[:, :, :-2], tp[:, :, 1:-1], tp[:, :, 2:]], axis=3)

    kw, vw = band(k), band(v)
    s_win = jnp.einsum('bgrnqd,bgnkd->bgrnqk', qb, kw).astype(jnp.float32) * scale
    blk_id = np.arange(nb)[:, None, None]
    qpos = blk_id * BLK + np.arange(BLK)[None, :, None]
    kpos = (blk_id - 1) * BLK + np.arange(3 * BLK)[None, None, :]
    valid = (np.abs(qpos - kpos) <= WINDOW) & (kpos >= 0) & (kpos < l)
    s_win = jnp.where(valid, s_win, NEG_INF)
    s_ctx = jnp.einsum('bgrnqd,bgkd->bgrnqk', qb, k_ctx).astype(jnp.float32) * scale
    s_sink = jnp.broadcast_to(sink.astype(jnp.float32).reshape(g, rep, 1, 1, 1), s_ctx.shape[:-1] + (1,))
    prob = softmax_f32(jnp.concatenate([s_win, s_ctx, s_sink], axis=-1))
    n_ctx = k_ctx.shape[2]
    p_win = prob[..., :3 * BLK]
    p_ctx = prob[..., 3 * BLK:3 * BLK + n_ctx]
    o = jnp.einsum('bgrnqk,bgnkd->bgrnqd', p_win, vw) + jnp.einsum('bgrnqk,bgkd->bgrnqd', p_ctx, v_ctx)
    return o.reshape(b, ha, l, dh).astype(q.dtype)


def ctx_gqa_attention(q, k, v, sink):
    b, ha, n, dh = q.shape
    g = k.shape[1]
    rep = ha // g
    qg = q.reshape(b, g, rep, n, dh)
    s = jnp.einsum('bgrqd,bgkd->bgrqk', qg, k).astype(jnp.float32) * dh ** -0.5
    s_sink = jnp.broadcast_to(sink.astype(jnp.float32).reshape(g, rep, 1, 1), (b, g, rep, n, 1))
    prob = softmax_f32(jnp.concatenate([s, s_sink], axis=-1))[..., :-1]
    o = jnp.einsum('bgrqk,bgkd->bgrqd', prob, v)
    return o.reshape(b, ha, n, dh).astype(q.dtype)


def diff_probs(q, k, lam):
    s = jnp.einsum('bhjqd,bhjkd->bhjqk', q, k).astype(jnp.float32) * q.shape[-1] ** -0.5
    prob = softmax_f32(s)
    return prob[:, :, 0] - lam * prob[:, :, 1]


def diff_attention(q, k, v, k_ctx, v_ctx, lam):
    b, h, _, l, dk = q.shape
    nb = l // BLK
    k_all = jnp.concatenate([k, k_ctx], axis=3)
    v_all = jnp.concatenate([v, v_ctx], axis=2)
    q_blocks = jnp.moveaxis(q.reshape(b, h, 2, nb, BLK, dk), 3, 0)

    def one_block(qb):
        return jnp.einsum('bhqk,bhkd->bhqd', diff_probs(qb, k_all, lam), v_all)

    o = lax.map(one_block, q_blocks)
    return jnp.moveaxis(o, 0, 2).reshape(b, h, l, v.shape[-1])


def retention_scan(q, k, v, log_gamma, s0):
    b, h, l, dk = q.shape
    dv = v.shape[-1]
    n_chunks = l // RET_CHUNK
    idx = jnp.arange(RET_CHUNK, dtype=jnp.float32)
    lg = log_gamma[:, None]
    rel = idx[:, None] - idx[None, :]
    inner_decay = jnp.exp(jnp.where(rel[None] >= 0, rel[None] * lg[..., None], -jnp.inf))
    q_decay = jnp.exp((idx + 1.0)[None, :] * lg)[..., None]
    k_decay = jnp.exp((RET_CHUNK - 1.0 - idx)[None, :] * lg)[..., None]
    chunk_decay = jnp.exp(RET_CHUNK * lg)[..., None]

    def chunks(t):
        return jnp.moveaxis(t.reshape(b, h, n_chunks, RET_CHUNK, t.shape[-1]), 2, 0)

    def step(state, inp):
        qi, ki, vi = inp
        inner = jnp.einsum('bhid,bhjd->bhij', qi, ki) * inner_decay
        o = jnp.einsum('bhij,bhjv->bhiv', inner, vi) + jnp.einsum('bhid,bhdv->bhiv', qi * q_decay, state)
        state = state * chunk_decay + jnp.einsum('bhjd,bhjv->bhdv', ki * k_decay, vi)
        return state, o

    s_final, o = lax.scan(step, s0, (chunks(q), chunks(k), chunks(v)))
    return jnp.moveaxis(o, 0, 2).reshape(b, h, l, dv), s_final


def neighbourhood_attention(q, k, v, k_ctx, v_ctx, rpb):
    b, h, l, dh = q.shape
    rows = l // GRID_W
    kh = min(NA_ROWS, rows)
    ncb = GRID_W // NA_QCB
    scale = dh ** -0.5
    r = np.arange(rows)
    row_idx = np.clip(r - kh // 2, 0, rows - kh)[:, None] + np.arange(kh)[None, :]
    col_idx = (np.clip(np.arange(ncb) * NA_QCB - NA_COLS // 2, 0, GRID_W - NA_KCB)[:, None]
               + np.arange(NA_KCB)[None, :])
    qcol = np.arange(ncb)[:, None] * NA_QCB + np.arange(NA_QCB)[None, :]
    cstart = np.clip(qcol - NA_COLS // 2, 0, GRID_W - NA_COLS)[..., None]
    kcol = col_idx[:, None, :]
    col_valid = (kcol >= cstart) & (kcol < cstart + NA_COLS)
    dr = row_idx - r[:, None] + NA_ROWS - 1
    dc = np.clip(kcol - qcol[..., None] + NA_COLS - 1, 0, 2 * NA_COLS - 2)
    bias = rpb[:, dr[:, None, None, :, None], dc[None, :, :, None, :]]

    def gather(t):
        tg = t.reshape(b, h, rows, GRID_W, dh)
        return tg[:, :, row_idx[:, None, :, None], col_idx[None, :, None, :]]

    kg, vg = gather(k), gather(v)
    qg = q.reshape(b, h, rows, ncb, NA_QCB, dh)
    s = jnp.einsum('bhrjqd,bhrjkwd->bhrjqkw', qg, kg).astype(jnp.float32) * scale + bias.astype(jnp.float32)
    s = jnp.where(col_valid[:, :, None, :], s, NEG_INF).reshape(b, h, rows, ncb, NA_QCB, kh * NA_KCB)
    s_ctx = jnp.einsum('bhrjqd,bhkd->bhrjqk', qg, k_ctx).astype(jnp.float32) * scale
    prob = softmax_f32(jnp.concatenate([s, s_ctx], axis=-1))
    p_nb = prob[..., :kh * NA_KCB].reshape(b, h, rows, ncb, NA_QCB, kh, NA_KCB)
    p_ctx = prob[..., kh * NA_KCB:]
    o = jnp.einsum('bhrjqkw,bhrjkwd->bhrjqd', p_nb, vg) + jnp.einsum('bhrjqk,bhkd->bhrjqd', p_ctx, v_ctx)
    return o.reshape(b, h, l, dh).astype(q.dtype)


def dense_ctx_attention(q, k, v):
    s = jnp.einsum('bhqd,bhkd->bhqk', q, k).astype(jnp.float32) * q.shape[-1] ** -0.5
    return jnp.einsum('bhqk,bhkd->bhqd', softmax_f32(s), v).astype(q.dtype)


def squared_relu_mlp(h, w1, w2):
    return jnp.square(jax.nn.relu(h @ w1)) @ w2


def hybrid_layer(x, ctx, c, c_ctx, p, layer_idx, need_ctx):
    f32 = jnp.float32
    b, l, _ = x.shape
    pos = jnp.arange(l)
    rows = (pos // GRID_W).astype(f32)
    cols = (pos % GRID_W).astype(f32)

    mod_x = (jax.nn.silu(c) @ p['w_mod'] + p['b_mod'])[:, None, :]
    mod_c = (jax.nn.silu(c_ctx) @ p['w_mod'] + p['b_mod'])[None, None, :]
    sh1x, sc1x, g1x, sh2x, sc2x, g2x = jnp.split(mod_x, 6, axis=-1)
    sh1c, sc1c, g1c, sh2c, sc2c, g2c = jnp.split(mod_c, 6, axis=-1)

    hx = rms_norm(x, p['norm1_g']) * (1 + sc1x) + sh1x
    hc = rms_norm(ctx, p['norm1_g']) * (1 + sc1c) + sh1c
    (lw_q, lw_k, lw_v, lf_q, lf_k, lf_v, lr_q, lr_k, lr_v, lr_g, ln_q, ln_k, ln_v) = split_in(hx @ p['w_in'])
    (cw_q, cw_k, cw_v, cf_q, cf_k, cf_v, cr_q, cr_k, cr_v, cr_g, cn_q, cn_k, cn_v) = split_in(hc @ p['w_in'])

    q = rope_2d(rms_norm(to_heads(lw_q, A_HEADS, HEAD_DIM), p['a_qnorm_g']), rows, cols)
    k = rope_2d(rms_norm(to_heads(lw_k, A_KV_HEADS, HEAD_DIM), p['a_knorm_g']), rows, cols)
    v = to_heads(lw_v, A_KV_HEADS, HEAD_DIM)
    k_c = rms_norm(to_heads(cw_k, A_KV_HEADS, HEAD_DIM), p['a_knorm_g'])
    v_c = to_heads(cw_v, A_KV_HEADS, HEAD_DIM)
    out_w = window_attention(q, k, v, k_c, v_c, p['a_sink'])
    if need_ctx:
        q_c = rms_norm(to_heads(cw_q, A_HEADS, HEAD_DIM), p['a_qnorm_g'])
        cout_w = ctx_gqa_attention(q_c, k_c, v_c, p['a_sink'])

    lambda_init = 0.8 - 0.6 * math.exp(-0.3 * layer_idx)
    lam = (jnp.exp(jnp.sum(p['b_lambda_q1'].astype(f32) * p['b_lambda_k1'].astype(f32)))
           - jnp.exp(jnp.sum(p['b_lambda_q2'].astype(f32) * p['b_lambda_k2'].astype(f32))) + lambda_init)
    q = rope_2d(rms_norm(to_diff_heads(lf_q), p['b_qnorm_g']), rows, cols)
    k = rope_2d(rms_norm(to_diff_heads(lf_k), p['b_knorm_g']), rows, cols)
    v = to_heads(lf_v, B_HEADS, HEAD_DIM)
    k_c = rms_norm(to_diff_heads(cf_k), p['b_knorm_g'])
    v_c = to_heads(cf_v, B_HEADS, HEAD_DIM)
    out_f = (rms_norm(diff_attention(q, k, v, k_c, v_c, lam), p['b_subln_g']) * (1 - lambda_init)).astype(x.dtype)
    if need_ctx:
        q_c = rms_norm(to_diff_heads(cf_q), p['b_qnorm_g'])
        o_c = jnp.einsum('bhqk,bhkd->bhqd', diff_probs(q_c, k_c, lam), v_c)
        cout_f = (rms_norm(o_c, p['b_subln_g']) * (1 - lambda_init)).astype(x.dtype)

    lg_f = jax.nn.log_sigmoid(p['c_decay_fwd'].astype(f32))
    lg_b = jax.nn.log_sigmoid(p['c_decay_bwd'].astype(f32))
    k_scale = HEAD_DIM ** -0.5
    flip = lambda t: jnp.flip(t, axis=2)
    q = to_heads(lr_q, C_HEADS, HEAD_DIM)
    k = to_heads(lr_k, C_HEADS, HEAD_DIM) * k_scale
    v = to_heads(lr_v, C_HEADS, HEAD_DIM)
    q_c = to_heads(cr_q, C_HEADS, HEAD_DIM)
    k_c = to_heads(cr_k, C_HEADS, HEAD_DIM) * k_scale
    v_c = to_heads(cr_v, C_HEADS, HEAD_DIM)
    s0 = jnp.zeros((b, C_HEADS, HEAD_DIM, HEAD_DIM), f32)
    o_cf, s_f = retention_scan(q_c, k_c, v_c, lg_f, s0)
    o_cb, s_b = retention_scan(flip(q_c), flip(k_c), flip(v_c), lg_b, s0)
    o_f, _ = retention_scan(q, k, v, lg_f, s_f)
    o_b, _ = retention_scan(flip(q), flip(k), flip(v), lg_b, s_b)
    gate = jax.nn.silu(to_heads(lr_g, C_HEADS, HEAD_DIM))
    out_r = (rms_norm(o_f + flip(o_b), p['c_gn_g']) * gate).astype(x.dtype)
    if need_ctx:
        gate_c = jax.nn.silu(to_heads(cr_g, C_HEADS, HEAD_DIM))
        cout_r = (rms_norm(o_cf + flip(o_cb), p['c_gn_g']) * gate_c).astype(x.dtype)

    q = rms_norm(to_heads(ln_q, D_HEADS, HEAD_DIM), p['d_qnorm_g'])
    k = rms_norm(to_heads(ln_k, D_HEADS, HEAD_DIM), p['d_knorm_g'])
    v = to_heads(ln_v, D_HEADS, HEAD_DIM)
    k_c = rms_norm(to_heads(cn_k, D_HEADS, HEAD_DIM), p['d_knorm_g'])
    v_c = to_heads(cn_v, D_HEADS, HEAD_DIM)
    out_n = neighbourhood_attention(q, k, v, k_c, v_c, p['d_rpb'])
    if need_ctx:
        q_c = rms_norm(to_heads(cn_q, D_HEADS, HEAD_DIM), p['d_qnorm_g'])
        cout_n = dense_ctx_attention(q_c, k_c, v_c)

    mix_x = jnp.concatenate([from_heads(out_w), from_heads(out_f), from_heads(out_r), from_heads(out_n)], axis=-1) @ p['w_out']
    x = x + (g1x * mix_x).astype(x.dtype)
    h2 = rms_norm(x, p['norm2_g']) * (1 + sc2x) + sh2x
    x = x + (g2x * squared_relu_mlp(h2, p['w_mlp1'], p['w_mlp2'])).astype(x.dtype)

    if need_ctx:
        mix_c = jnp.concatenate([from_heads(cout_w), from_heads(cout_f), from_heads(cout_r), from_heads(cout_n)], axis=-1) @ p['w_out']
        ctx = ctx + (g1c * mix_c).astype(ctx.dtype)
        h2c = rms_norm(ctx, p['norm2_g']) * (1 + sc2c) + sh2c
        ctx = ctx + (g2c * squared_relu_mlp(h2c, p['w_mlp1'], p['w_mlp2'])).astype(ctx.dtype)
    return x, ctx


def setup_inputs(seed: int = 0) -> dict:
    key = jax.random.key(seed)
    ks = iter(jax.random.split(key, 32))
    f32 = jnp.float32

    def nrm(shape, scale):
        return jax.random.normal(next(ks), shape, f32) * scale

    def gain(shape):
        return 1.0 + 0.05 * jax.random.normal(next(ks), shape, f32)

    decay_base = jnp.log(2.0 ** (5.0 + jnp.arange(C_HEADS, dtype=f32)) - 1.0)
    return {
        'x': nrm((BATCH, SEQ, D_MODEL), 1.0),
        'c': nrm((BATCH, D_MODEL), 1.0),
        'ctx': nrm((BATCH, CTX_LEN, D_MODEL), 1.0),
        'c_ctx': nrm((D_MODEL,), 1.0),
        'w_mod': nrm((DEPTH, D_MODEL, 6 * D_MODEL), 0.5 * D_MODEL ** -0.5),
        'b_mod': nrm((DEPTH, 6 * D_MODEL), 0.02),
        'norm1_g': gain((DEPTH, D_MODEL)),
        'norm2_g': gain((DEPTH, D_MODEL)),
        'w_in': nrm((DEPTH, D_MODEL, IN_WIDTH), D_MODEL ** -0.5),
        'w_out': nrm((DEPTH, N_MIXERS * GROUP_WIDTH, D_MODEL), (N_MIXERS * GROUP_WIDTH) ** -0.5),
        'a_qnorm_g': gain((DEPTH, HEAD_DIM)),
        'a_knorm_g': gain((DEPTH, HEAD_DIM)),
        'a_sink': nrm((DEPTH, A_HEADS), 0.5),
        'b_qnorm_g': gain((DEPTH, B_QK_DIM)),
        'b_knorm_g': gain((DEPTH, B_QK_DIM)),
        'b_lambda_q1': nrm((DEPTH, B_QK_DIM), 0.1),
        'b_lambda_k1': nrm((DEPTH, B_QK_DIM), 0.1),
        'b_lambda_q2': nrm((DEPTH, B_QK_DIM), 0.1),
        'b_lambda_k2': nrm((DEPTH, B_QK_DIM), 0.1),
        'b_subln_g': gain((DEPTH, HEAD_DIM)),
        'c_decay_fwd': decay_base[None, :] + nrm((DEPTH, C_HEADS), 0.1),
        'c_decay_bwd': decay_base[None, :] + nrm((DEPTH, C_HEADS), 0.1),
        'c_gn_g': gain((DEPTH, HEAD_DIM)),
        'd_qnorm_g': gain((DEPTH, HEAD_DIM)),
        'd_knorm_g': gain((DEPTH, HEAD_DIM)),
        'd_rpb': nrm((DEPTH, D_HEADS, 2 * NA_ROWS - 1, 2 * NA_COLS - 1), 0.1),
        'w_mlp1': nrm((DEPTH, D_MODEL, D_FF), D_MODEL ** -0.5),
        'w_mlp2': nrm((DEPTH, D_FF, D_MODEL), D_FF ** -0.5),
    }


def reference(x, c, ctx, c_ctx, w_mod, b_mod, norm1_g, norm2_g, w_in, w_out,
              a_qnorm_g, a_knorm_g, a_sink, b_qnorm_g, b_knorm_g,
              b_lambda_q1, b_lambda_k1, b_lambda_q2, b_lambda_k2, b_subln_g,
              c_decay_fwd, c_decay_bwd, c_gn_g, d_qnorm_g, d_knorm_g, d_rpb,
              w_mlp1, w_mlp2):
    for layer in range(DEPTH):
        p = {
            'w_mod': w_mod[layer], 'b_mod': b_mod[layer],
            'norm1_g': norm1_g[layer], 'norm2_g': norm2_g[layer],
            'w_in': w_in[layer], 'w_out': w_out[layer],
            'a_qnorm_g': a_qnorm_g[layer], 'a_knorm_g': a_knorm_g[layer], 'a_sink': a_sink[layer],
            'b_qnorm_g': b_qnorm_g[layer], 'b_knorm_g': b_knorm_g[layer],
            'b_lambda_q1': b_lambda_q1[layer], 'b_lambda_k1': b_lambda_k1[layer],
            'b_lambda_q2': b_lambda_q2[layer], 'b_lambda_k2': b_lambda_k2[layer],
            'b_subln_g': b_subln_g[layer],
            'c_decay_fwd': c_decay_fwd[layer], 'c_decay_bwd': c_decay_bwd[layer], 'c_gn_g': c_gn_g[layer],
            'd_qnorm_g': d_qnorm_g[layer], 'd_knorm_g': d_knorm_g[layer], 'd_rpb': d_rpb[layer],
            'w_mlp1': w_mlp1[layer], 'w_mlp2': w_mlp2[layer],
        }
        x, ctx = hybrid_layer(x, ctx, c, c_ctx, p, layer, layer < DEPTH - 1)
    return x
```

```python
import math
from contextlib import ExitStack
import numpy as np
import concourse.bass as bass
import concourse.mybir as mybir
from concourse.bass_utils import run_bass_kernel_spmd

F32 = mybir.dt.float32
BF16 = mybir.dt.bfloat16
ALU = mybir.AluOpType
AF = mybir.ActivationFunctionType
AX = mybir.AxisListType

D = 1024
T = 4096
NCTX = 256
NK = T + NCTX
DEPTH = 2
NB = 4
EPS = 1e-6
NEG = -30000.0
NPP = 216
NCST = 1730
TM0 = 2176
NCOL = 3328


class Buf:
    __slots__ = ("name", "t", "lw", "rd", "dsem", "dcnt", "space")

    def __init__(self, name, t, space):
        self.name = name
        self.t = t
        self.space = space
        self.lw = None
        self.rd = []
        self.dsem = None
        self.dcnt = 0

    def __getitem__(self, key):
        return self.t[key]


class KB:
    SEM_LIMIT = 32000

    def __init__(self, nc):
        self.nc = nc
        self.engs = {"pe": nc.tensor, "act": nc.scalar, "dve": nc.vector,
                     "pool": nc.gpsimd, "sp": nc.sync}
        self.esem = {}
        self.ecnt = {}
        self.nsem = 0
        self.retired = []
        self.free_dsems = []
        self.pe_sems = set()
        for e in self.engs:
            self._new_esem(e)
        self.seen = {e: {} for e in self.engs}
        self.dsems = []
        self.nwaits = 0
        self.ninst = 0

    def _alloc_sem(self, name):
        self.nsem += 1
        return self.nc.alloc_semaphore(f"{name}_{self.nsem}")

    def _new_esem(self, e):
        if e in self.esem and self.ecnt[e] > 0:
            self.retired.append((self.esem[e], self.ecnt[e]))
        self.esem[e] = self._alloc_sem("e" + e)
        self.ecnt[e] = 0
        if e == "pe":
            self.pe_sems.add(self.esem[e].num)

    def sbuf(self, st, name, shape, dtype):
        self.nbuf = getattr(self, "nbuf", 0) + 1
        t = st.enter_context(self.nc.sbuf_tensor(f"s{self.nbuf}_{name}", list(shape), dtype))
        b = Buf(name, t, "sbuf")
        st.callback(self.release, b)
        return b

    def psum_all(self):
        t = self.nc.alloc_psum_tensor("psum_all", [128, 4096], F32)
        return t, [Buf(f"bank{i}", t, "psum") for i in range(8)]

    def dram(self, name, shape, dtype, kind="Internal"):
        t = self.nc.dram_tensor(name, list(shape), dtype, kind=kind)
        return Buf(name, t, "dram")

    def release(self, b):
        if b.dsem is not None:
            self.free_dsems.append((b.dsem, b.dcnt))
            if b in self.dsems:
                self.dsems.remove(b)
            self.retired.append((b.dsem, b.dcnt))
            b.dsem = None

    def _wait(self, e, deps):
        eng = self.engs[e]
        seen = self.seen[e]
        best = {}
        for d in deps:
            if d is None:
                continue
            s, v = d
            if e == "pe" and s.num in self.pe_sems:
                continue
            if best.get(s.num, (None, 0))[1] < v:
                best[s.num] = (s, v)
        for s, v in best.values():
            if seen.get(s.num, 0) < v:
                eng.wait_ge(s, v)
                seen[s.num] = v
                self.nwaits += 1

    def _deps(self, reads, writes, e=None):
        deps = []
        for b in reads:
            deps.append(b.lw)
            if b.space == "psum" and e is not None:
                mine = self.esem[e].num
                deps.extend(t for t in b.rd if t[0].num != mine)
        for b in writes:
            deps.append(b.lw)
            deps.extend(b.rd)
        return deps

    def _commit(self, tok, reads, writes):
        for b in reads:
            b.rd.append(tok)
            if len(b.rd) > 48:
                m = {}
                for s, v in b.rd:
                    if m.get(s.num, (None, 0))[1] < v:
                        m[s.num] = (s, v)
                b.rd = list(m.values())
        for b in writes:
            b.lw = tok
            b.rd = []

    def op(self, e, fn, reads=(), writes=()):
        if self.ecnt[e] >= self.SEM_LIMIT:
            self._new_esem(e)
        self._wait(e, self._deps(reads, writes, e))
        ins = fn(self.engs[e])
        self.ecnt[e] += 1
        ins.then_inc(self.esem[e], 1)
        tok = (self.esem[e], self.ecnt[e])
        self._commit(tok, reads, writes)
        self.ninst += 1
        return tok

    def dma(self, q, out_ap, in_ap, reads=(), writes=()):
        owner = None
        for b in list(writes) + list(reads):
            if b.space == "sbuf":
                owner = b
                break
        if owner is None:
            owner = (list(writes) + list(reads))[0]
        if owner.dsem is None or owner.dcnt >= self.SEM_LIMIT:
            if owner.dsem is not None:
                self.retired.append((owner.dsem, owner.dcnt))
                owner.dsem = None
            while self.free_dsems:
                s, c = self.free_dsems.pop()
                if c < self.SEM_LIMIT - 2000:
                    owner.dsem, owner.dcnt = s, c
                    break
            if owner.dsem is None:
                owner.dsem = self._alloc_sem("d")
                owner.dcnt = 0
            if owner not in self.dsems:
                self.dsems.append(owner)
        self._wait(q, self._deps(reads, writes))
        ins = self.engs[q].dma_start(out=out_ap, in_=in_ap)
        owner.dcnt += 16
        ins.then_inc(owner.dsem, 16)
        tok = (owner.dsem, owner.dcnt)
        self._commit(tok, reads, writes)
        self.ninst += 1
        return tok

    def fence(self, engines=None):
        m = {}
        for s, v in self.retired:
            if m.get(s.num, (None, 0))[1] < v:
                m[s.num] = (s, v)
        for e in self.engs:
            if self.ecnt[e] > 0:
                m[self.esem[e].num] = (self.esem[e], self.ecnt[e])
        for b in self.dsems:
            if b.dcnt > 0 and m.get(b.dsem.num, (None, 0))[1] < b.dcnt:
                m[b.dsem.num] = (b.dsem, b.dcnt)
        targets = list(m.values())
        for e in (engines or self.engs):
            self._wait(e, targets)
        if engines is None:
            self.retired = []


def _partner(dh):
    half, q = dh // 2, dh // 4
    p = np.arange(dh)
    loc = p % half
    return np.where(loc < q, p + q, p - q), np.where(loc < q, -1.0, 1.0).astype(np.float32)


def _rope_tables(dh, nrep):
    half, q = dh // 2, dh // 4
    pos = np.arange(T)
    rows = (pos // 64).astype(np.float32)
    cols = (pos % 64).astype(np.float32)
    freqs = (10000.0 ** (-np.arange(q, dtype=np.float32) / q)).astype(np.float32)
    _, sgn = _partner(dh)
    cos = np.zeros((dh, T), np.float32)
    sin = np.zeros((dh, T), np.float32)
    for d in range(dh):
        pv = rows if d < half else cols
        ang = (pv * freqs[d % q]).astype(np.float32)
        cos[d] = np.cos(ang)
        sin[d] = np.sin(ang) * sgn[d]
    return np.stack([np.tile(cos, (nrep, 1)), np.tile(sin, (nrep, 1))]).astype(np.float32)


def _fm_groups():
    p64, _ = _partner(64)
    p32, _ = _partner(32)
    g = []
    ar = np.arange
    for nm, c0 in (("Aq0", 0), ("Aq1", 128), ("Ak", 256)):
        g.append((nm, c0 + ar(128)))
    for h in range(4):
        g.append((f"Bq{h}", 512 + h * 64 + ar(64)))
    for h in range(4):
        g.append((f"Bk{h}", 768 + h * 64 + ar(64)))
    for nm, c0 in (("Cq0", 1280), ("Cq1", 1408), ("Ck0", 1536), ("Ck1", 1664), ("Cg0", 2048), ("Cg1", 2176),
                   ("Dq0", 2304), ("Dq1", 2432), ("Dk0", 2560), ("Dk1", 2688)):
        g.append((nm, c0 + ar(128)))
    return g


FMG = _fm_groups()
FMOFF = {}
_o = 0
for _n, _c in FMG:
    FMOFF[_n] = (_o, len(_c))
    _o += len(_c)
assert _o == TM0
TMCOLS = np.concatenate([1024 + np.arange(256), 2816 + np.arange(256), 1792 + np.arange(256),
                         1536 + np.arange(256), 384 + np.arange(128)])


def _dtables(rpb):
    kc = np.arange(64)[:, None]
    c = np.arange(64)[None, :]
    cstart = np.clip(c - 8, 0, 48)
    colv = (kc >= cstart) & (kc < cstart + 16)
    dc = np.clip(kc - c + 15, 0, 30)
    bias = np.zeros((DEPTH, 4, 128, 12, 128), np.float32)
    mask = np.full((128, 12, 128), NEG, np.float32)
    for v in range(12):
        delta = v - 2 if v < 5 else v - 8
        for krp in range(2):
            for rp in range(2):
                dr = 2 * delta + krp - rp
                ok = (-4 <= dr <= 3) if v < 5 else (abs(dr) <= 7)
                if abs(dr) > 7:
                    continue
                bias[:, :, krp * 64:(krp + 1) * 64, v, rp * 64:(rp + 1) * 64] = rpb[:, :, dr + 7][:, :, dc]
                if ok:
                    mask[krp * 64:(krp + 1) * 64, v, rp * 64:(rp + 1) * 64] = np.where(colv, 0.0, NEG)
    return bias, mask


def _consts():
    c = np.zeros((128, NCST), np.float32)
    i = np.arange(128)
    c[:, 0:128] = np.eye(128)
    c[:, 128:256] = (i[:, None] // 64 == i[None, :] // 64)
    c[:, 256:384] = (i[:, None] // 32 == i[None, :] // 32)
    c[:, 384:512] = 1.0
    j, q = i[:, None], i[None, :]
    c[:, 512:640] = np.where(j >= q, 0.0, NEG)
    c[:, 640:768] = np.where(j <= q, 0.0, NEG)
    c[:, 768:896] = q - j
    c[:, 896:1024] = (q >= j)
    c[:, 1024:1152] = j - q
    c[:, 1152:1280] = (j >= q)
    c[:, 1280:1408] = q + 1
    c[:, 1408:1536] = 128 - q
    c[:, 1536] = 127 - i
    c[:, 1537] = i
    p64, _ = _partner(64)
    p32, _ = _partner(32)
    for m in range(128):
        c[(m // 64) * 64 + p64[m % 64], 1538 + m] = 1.0
    for m in range(64):
        c[(m // 32) * 32 + p32[m % 32], 1666 + m] = 1.0
    return c


def _prep(inp):
    f = lambda a: np.ascontiguousarray(a, dtype=np.float32)
    x, c, ctx, c_ctx = inp["x"], inp["c"], inp["ctx"], inp["c_ctx"]
    shared = {}
    shared["w_mod_r"] = f(inp["w_mod"].reshape(DEPTH, 8, 128, 12, 512).transpose(0, 3, 2, 1, 4))
    fmcols = np.concatenate([cc for _, cc in FMG] + [TMCOLS])
    shared["w_in_r"] = f(inp["w_in"][:, :, fmcols].reshape(DEPTH, 8, 128, 32, NCOL // 32).transpose(0, 3, 2, 1, 4))
    shared["w_out_r"] = f(inp["w_out"].reshape(DEPTH, 8, 128, D).transpose(0, 2, 1, 3))
    shared["w1_r"] = f(inp["w_mlp1"].reshape(DEPTH, 8, 128, 4 * D).transpose(0, 2, 1, 3))
    shared["w2_r"] = f(inp["w_mlp2"].reshape(DEPTH, 32, 128, D).transpose(0, 2, 1, 3))
    pp = np.zeros((DEPTH, 128, NPP), np.float32)
    p = np.arange(128)
    p64, _ = _partner(64)
    p32, _ = _partner(32)
    for l in range(DEPTH):
        pp[l, :, 0:8] = inp["norm1_g"][l].reshape(8, 128).T
        pp[l, :, 8:16] = inp["norm2_g"][l].reshape(8, 128).T
        pp[l, :, 16:64] = inp["b_mod"][l].reshape(48, 128).T
        pp[l, :, 64] = inp["a_qnorm_g"][l][p % 64]
        pp[l, :, 65] = inp["a_qnorm_g"][l][p64[p % 64]]
        pp[l, :, 66] = inp["a_knorm_g"][l][p % 64]
        pp[l, :, 67] = inp["a_knorm_g"][l][p64[p % 64]]
        pp[l, :, 68] = inp["b_qnorm_g"][l][p % 32]
        pp[l, :, 69] = inp["b_qnorm_g"][l][p32[p % 32]]
        pp[l, :, 70] = inp["b_knorm_g"][l][p % 32]
        pp[l, :, 71] = inp["b_knorm_g"][l][p32[p % 32]]
        pp[l, :, 72] = inp["d_qnorm_g"][l][p % 64]
        pp[l, :, 73] = inp["d_knorm_g"][l][p % 64]
        pp[l, :, 74] = inp["c_gn_g"][l][p % 64]
        pp[l, :, 75] = inp["b_subln_g"][l][p % 64]
        pp[l, :, 76:80] = inp["a_sink"][l][None, :]
        pp[l, :, 80:84] = inp["c_decay_fwd"][l][None, :]
        pp[l, :, 84:88] = inp["c_decay_bwd"][l][None, :]
        pp[l, :, 88:120] = inp["b_lambda_q1"][l][None, :]
        pp[l, :, 120:152] = inp["b_lambda_k1"][l][None, :]
        pp[l, :, 152:184] = inp["b_lambda_q2"][l][None, :]
        pp[l, :, 184:216] = inp["b_lambda_k2"][l][None, :]
    shared["pp"] = pp
    dbias, dmask = _dtables(np.asarray(inp["d_rpb"], np.float32))
    shared["dbias"] = dbias
    shared["dmask"] = dmask
    shared["cst"] = _consts()
    shared["ropeA"] = _rope_tables(64, 2)
    shared["ropeB"] = _rope_tables(32, 2)
    maps = []
    for b in range(NB):
        m = dict(shared)
        m["xin"] = f(np.concatenate([x[b].T, ctx[b].T], axis=1))
        cv = np.stack([c[b].reshape(8, 128).T, c_ctx.reshape(8, 128).T], axis=-1)
        m["cvec"] = f(cv)
        maps.append(m)
    return maps


def build(nlayers=DEPTH, debug=False, only=None):
    nc = bass.Bass("TRN2", target_bir_lowering=False)
    kb = KB(nc)
    EI = "ExternalInput"
    xin = kb.dram("xin", [D, NK], F32, EI)
    cvec_d = kb.dram("cvec", [128, 8, 2], F32, EI)
    w_mod_d = kb.dram("w_mod_r", [DEPTH, 12, 128, 8, 512], F32, EI)
    w_in_d = kb.dram("w_in_r", [DEPTH, 32, 128, 8, NCOL // 32], F32, EI)
    w_out_d = kb.dram("w_out_r", [DEPTH, 128, 8, D], F32, EI)
    w1_d = kb.dram("w1_r", [DEPTH, 128, 8, 4 * D], F32, EI)
    w2_d = kb.dram("w2_r", [DEPTH, 128, 32, D], F32, EI)
    pp_d = kb.dram("pp", [DEPTH, 128, NPP], F32, EI)
    dbias_d = kb.dram("dbias", [DEPTH, 4, 128, 12, 128], F32, EI)
    dmask_d = kb.dram("dmask", [128, 12, 128], F32, EI)
    cst_d = kb.dram("cst", [128, NCST], F32, EI)
    ropeA_d = kb.dram("ropeA", [2, 128, T], F32, EI)
    ropeB_d = kb.dram("ropeB", [2, 64, T], F32, EI)
    outT = kb.dram("outT", [D, T], F32, "ExternalOutput")
    xmid = kb.dram("xmid", [D, NK], F32, "ExternalOutput" if debug else "Internal")
    attnT = kb.dram("attnT", [D, NK], BF16)
    dbg_attn = kb.dram("dbg_attn", [D, NK], F32, "ExternalOutput") if debug else None
    wo_bf = kb.dram("wo_bf", [128, 8, D], BF16)
    w1_bf = kb.dram("w1_bf", [32, 128, 8, 128], BF16)
    w2_bf = kb.dram("w2_bf", [8, 128, 32, 128], BF16)
    fm = {}
    for n, cc in FMG:
        if n.endswith("r"):
            continue
        fm[n] = kb.dram("fm_" + n, [len(cc), NK], BF16)
    VA_d = kb.dram("VA_d", [NK, 2, 128], BF16)
    VB_d = kb.dram("VB_d", [NK, 4, 128], BF16)
    VD_d = kb.dram("VD_d", [NK, 4, 128], BF16)
    VC_d = kb.dram("VC_d", [NK, 256], BF16)
    KF_d = kb.dram("KF_d", [NK, 256], BF16)
    KBk_d = kb.dram("KBk_d", [NK, 256], BF16)

    pst, PB = kb.psum_all()

    def bank(i, p0=0, p1=128, w=512):
        return pst[p0:p1, i * 512:i * 512 + w]

    top = ExitStack()
    cst = kb.sbuf(top, "cst", [128, NCST], F32)
    cb = kb.sbuf(top, "cstbf", [128, 768], BF16)
    pp = kb.sbuf(top, "pp", [128, NPP], F32)
    der = kb.sbuf(top, "der", [128, 64], F32)
    modT = kb.sbuf(top, "modT", [128, 48, 2], F32)
    gm = kb.sbuf(top, "gm", [128, 2, 8, 2], F32)
    DT = kb.sbuf(top, "DT", [128, 8, 128], F32)
    DQ = kb.sbuf(top, "DQ", [64, 8, 128], F32)
    G128 = kb.sbuf(top, "G128", [64, 8, 64], F32)
    kdec = kb.sbuf(top, "kdec", [128, 8], F32)
    kb.dma("sp", cst[:], cst_d[:], reads=[cst_d], writes=[cst])
    kb.op("dve", lambda e: e.tensor_copy(cb[:], cst[:, 0:768]), reads=[cst], writes=[cb])
    permbf = kb.sbuf(top, "permbf", [128, 192], BF16)
    kb.op("dve", lambda e: e.tensor_copy(permbf[:], cst[:, 1538:1730]), reads=[cst], writes=[permbf])
    ident = cb[:, 0:128]
    blk64 = cb[:, 128:256]
    blk32 = cb[0:64, 256:320]
    ones_bf = cb[:, 384:512]
    amask = {"prev": cb[:, 512:640], "next": cb[:, 640:768]}

    C_AQ, C_AQP, C_AK, C_AKP, C_BQ, C_BQP, C_BK, C_BKP, C_DQ, C_DK, C_CGN, C_BSUB = range(12)
    C_ESINK = 12
    C_LGF = 16
    C_NLAM = 24
    C_TMP = 32

    def V(e, fn, reads, writes):
        return kb.op(e, fn, reads, writes)

    def phase0(l, st):
        lam_init = 0.8 - 0.6 * math.exp(-0.3 * l)
        kb.dma("sp", pp[:], pp_d[l], reads=[pp_d], writes=[pp])
        cv = kb.sbuf(st, "cv", [128, 8, 2], F32)
        sc = kb.sbuf(st, "sc", [128, 8, 2], F32)
        kb.dma("sp", cv[:], cvec_d[:], reads=[cvec_d], writes=[cv])
        V("act", lambda e: e.activation(sc[:], cv[:], AF.Silu), [cv], [sc])
        wm = [kb.sbuf(st, f"wm{i}", [128, 8, 512], F32) for i in range(2)]
        modrow = kb.sbuf(st, "modrow", [2, 6 * D], F32)
        pm = pst[:, 0:96].rearrange("p (j s) -> p j s", s=2)
        for ch in range(12):
            w = wm[ch % 2]
            kb.dma("sp", w[:], w_mod_d[l, ch], reads=[w_mod_d], writes=[w])
            rb = 1 + ch % 2

            def mm(e, w=w, rb=rb):
                ins = None
                for kc in range(8):
                    ins = e.matmul(bank(rb, 0, 2, 512), sc[:, kc, :], w[:, kc, :], start=(kc == 0), stop=(kc == 7))
                return ins
            V("pe", mm, [w, sc], [PB[rb]])
            V("act", lambda e, ch=ch, rb=rb: e.copy(modrow[:, ch * 512:(ch + 1) * 512], bank(rb, 0, 2, 512)), [PB[rb]], [modrow])

        def flip(e):
            ins = None
            for j in range(48):
                ins = e.matmul(pm[:, j, :], modrow[:, j * 128:(j + 1) * 128], cst[0:2, 0:2], start=True, stop=True)
            return ins
        V("pe", flip, [modrow, cst], [PB[0]])
        V("dve", lambda e: e.tensor_tensor(modT[:], pm, pp[:, 16:64].unsqueeze(2).to_broadcast([128, 48, 2]), ALU.add),
          [PB[0], pp], [modT])
        for k, (gcol, mj) in enumerate(((0, 8), (8, 32))):
            V("dve", lambda e, k=k, mj=mj: e.tensor_scalar(gm[:, k], modT[:, mj:mj + 8, :], 1.0, None, ALU.add), [modT], [gm])
            V("dve", lambda e, k=k, gcol=gcol: e.tensor_tensor(
                gm[:, k], gm[:, k], pp[:, gcol:gcol + 8].unsqueeze(2).to_broadcast([128, 8, 2]), ALU.mult), [gm, pp], [gm])
        s64, s32 = 64 ** -0.5, 32 ** -0.5
        for dst, src, mul in ((C_AQ, 64, s64), (C_AQP, 65, s64), (C_AK, 66, 1.0), (C_AKP, 67, 1.0),
                              (C_BQ, 68, s32), (C_BQP, 69, s32), (C_BK, 70, 1.0), (C_BKP, 71, 1.0),
                              (C_DQ, 72, s64), (C_DK, 73, 1.0), (C_CGN, 74, 1.0), (C_BSUB, 75, 1.0 - lam_init)):
            V("dve", lambda e, dst=dst, src=src, mul=mul: e.tensor_scalar(
                der[:, dst:dst + 1], pp[:, src:src + 1], float(mul), None, ALU.mult), [pp], [der])
        V("act", lambda e: e.activation(der[:, C_ESINK:C_ESINK + 4], pp[:, 76:80], AF.Exp), [pp], [der])
        for k, (a0, b0) in enumerate(((88, 120), (152, 184))):
            V("dve", lambda e, a0=a0, b0=b0: e.tensor_tensor(der[:, C_TMP:C_TMP + 32], pp[:, a0:a0 + 32], pp[:, b0:b0 + 32], ALU.mult),
              [pp], [der])
            V("dve", lambda e, k=k: e.tensor_reduce(der[:, C_NLAM + 1 + k:C_NLAM + 2 + k], der[:, C_TMP:C_TMP + 32], AX.X, ALU.add),
              [der], [der])
        V("act", lambda e: e.activation(der[:, C_NLAM + 1:C_NLAM + 3], der[:, C_NLAM + 1:C_NLAM + 3], AF.Exp), [der], [der])
        V("dve", lambda e: e.tensor_tensor(der[:, C_NLAM:C_NLAM + 1], der[:, C_NLAM + 2:C_NLAM + 3], der[:, C_NLAM + 1:C_NLAM + 2], ALU.subtract),
          [der], [der])
        V("dve", lambda e: e.tensor_scalar(der[:, C_NLAM:C_NLAM + 1], der[:, C_NLAM:C_NLAM + 1], float(-lam_init), None, ALU.add),
          [der], [der])
        V("act", lambda e: e.activation(der[:, C_LGF:C_LGF + 8], pp[:, 80:88], AF.Exp, scale=-1.0), [pp], [der])
        V("act", lambda e: e.activation(der[:, C_LGF:C_LGF + 8], der[:, C_LGF:C_LGF + 8], AF.Ln, bias=1.0), [der], [der])
        V("dve", lambda e: e.tensor_scalar(der[:, C_LGF:C_LGF + 8], der[:, C_LGF:C_LGF + 8], -1.0, None, ALU.mult), [der], [der])
        for dr_ in range(2):
            rel = cst[:, 768:896] if dr_ == 0 else cst[:, 1024:1152]
            tri = cst[:, 896:1024] if dr_ == 0 else cst[:, 1152:1280]
            iq = cst[0:64, 1280:1408] if dr_ == 0 else cst[0:64, 1408:1536]
            for h in range(4):
                k = dr_ * 4 + h
                lgc = der[:, C_LGF + k:C_LGF + k + 1]
                V("act", lambda e, k=k, rel=rel, lgc=lgc: e.activation(DT[:, k, :], rel, AF.Exp, scale=lgc), [der, cst], [DT])
                V("dve", lambda e, k=k, tri=tri: e.tensor_tensor(DT[:, k, :], DT[:, k, :], tri, ALU.mult), [DT, cst], [DT])
                V("act", lambda e, k=k, iq=iq: e.activation(DQ[:, k, :], iq, AF.Exp, scale=der[0:64, C_LGF + k:C_LGF + k + 1]), [der, cst], [DQ])
                V("act", lambda e, k=k, dr_=dr_, lgc=lgc: e.activation(kdec[:, k:k + 1], cst[:, 1536 + dr_:1537 + dr_], AF.Exp, scale=lgc),
                  [der, cst], [kdec])
        V("dve", lambda e: e.tensor_scalar(kdec[:], kdec[:], 0.125, None, ALU.mult), [kdec], [kdec])
        V("act", lambda e: e.activation(der[:, C_TMP:C_TMP + 8], der[:, C_LGF:C_LGF + 8], AF.Exp, scale=128.0), [der], [der])
        V("dve", lambda e: e.tensor_copy(G128[:], der[0:64, C_TMP:C_TMP + 8].unsqueeze(2).to_broadcast([64, 8, 64])), [der], [G128])

    def phase1(l, st, xsrc):
        win = kb.sbuf(st, "win", [128, 8, NCOL], BF16)
        wst = [kb.sbuf(st, f"wst{i}", [128, 8, 104], F32) for i in range(2)]
        for i in range(32):
            w = wst[i % 2]
            kb.dma("sp", w[:], w_in_d[l, i], reads=[w_in_d], writes=[w])
            if i % 2 == 0:
                V("act", lambda e, w=w, i=i: e.copy(win[:, :, i * 104:(i + 1) * 104], w[:]), [w], [win])
            else:
                V("pool", lambda e, w=w, i=i: e.tensor_copy(win[:, :, i * 104:(i + 1) * 104], w[:]), [w], [win])
        xT = [kb.sbuf(st, f"xT{i}", [128, 8, 512], F32) for i in range(2)]
        sq = kb.sbuf(st, "sq", [128, 8, 512], BF16)
        hT = kb.sbuf(st, "hT", [128, 8, 512], BF16)
        rs = kb.sbuf(st, "rs", [128, 512], F32)
        tmp = [kb.sbuf(st, f"tmp{i}", [128, 512], F32) for i in range(2)]
        rsg = [kb.sbuf(st, f"rsg{i}", [128, 512], F32) for i in range(2)]
        ta = [kb.sbuf(st, f"ta{i}", [128, 512], F32) for i in range(2)]
        tb = [kb.sbuf(st, f"tb{i}", [128, 512], F32) for i in range(2)]
        sqg = [kb.sbuf(st, f"sqg{i}", [128, 512], BF16) for i in range(2)]
        qb16 = [kb.sbuf(st, f"qb16{i}", [128, 512], BF16) for i in range(2)]
        og = [kb.sbuf(st, f"og{i}", [128, 512], BF16) for i in range(4)]
        rA = [kb.sbuf(st, f"rA{i}", [128, 2, 512], F32) for i in range(2)]
        rB = [kb.sbuf(st, f"rB{i}", [64, 2, 512], F32) for i in range(2)]
        VAs = [kb.sbuf(st, f"VAs{i}", [128, 2, 128], BF16) for i in range(2)]
        VBs = [kb.sbuf(st, f"VBs{i}", [128, 4, 128], BF16) for i in range(2)]
        VDs = [kb.sbuf(st, f"VDs{i}", [128, 4, 128], BF16) for i in range(2)]
        VCs = [kb.sbuf(st, f"VCs{i}", [128, 256], BF16) for i in range(2)]
        KFs = [kb.sbuf(st, f"KFs{i}", [128, 256], BF16) for i in range(2)]
        KBs = [kb.sbuf(st, f"KBs{i}", [128, 256], BF16) for i in range(2)]
        for tl in VAs + VBs + VDs:
            V("pool", lambda e, tl=tl: e.memset(tl[:], 1.0), [], [tl])
        cnt = {"b": 0, "o": 0, "g": 0}

        def nb():
            cnt["b"] = (cnt["b"] + 1) % 8
            return cnt["b"]

        def nog():
            cnt["o"] = (cnt["o"] + 1) % 4
            return og[cnt["o"]]

        def proj(bi, col0, P, Wb):
            def f(e):
                ins = None
                for kc in range(8):
                    ins = e.matmul(bank(bi, 0, P, Wb), win[:, kc, col0:col0 + P], hT[:, kc, :Wb], start=(kc == 0), stop=(kc == 7))
                return ins
            V("pe", f, [win, hT], [PB[bi]])

        def store(name, o, P, c0, Wb):
            kb.dma("pool", fm[name][:, c0:c0 + Wb], o[0:P, :Wb], reads=[o], writes=[])

        pend = []

        def flush(keep=0):
            while len(pend) > keep:
                pend.pop(0)()

        def normgrp(name, P, dh, gcol, c0, Wb, rope=None, gpcol=None):
            k = cnt["g"] = (cnt["g"] + 1) % 2
            col0 = FMOFF[name][0]
            bo = nb()
            proj(bo, col0, P, Wb)
            po = bank(bo, 0, P, Wb)
            V("act", lambda e: e.activation(sqg[k][0:P, :Wb], po, AF.Square), [PB[bo]], [sqg[k]])
            br = None
            if rope is not None:
                V("act", lambda e: e.copy(qb16[k][0:P, :Wb], po), [PB[bo]], [qb16[k]])
            rbuf = rope_buf[0]

            def stage2():
                nonlocal br
                if rope is not None:
                    br = nb()
                    perm = permbf[:, 0:128] if P == 128 else permbf[0:64, 128:192]
                    V("pe", lambda e: e.matmul(bank(br, 0, P, Wb), perm, qb16[k][0:P, :Wb], start=True, stop=True), [qb16[k], permbf], [PB[br]])
                bs = nb()
                blk = blk64[0:P, 0:P] if dh == 64 else blk32
                V("pe", lambda e: e.matmul(bank(bs, 0, P, Wb), blk, sqg[k][0:P, :Wb], start=True, stop=True), [sqg[k], cb], [PB[bs]])
                V("act", lambda e: e.activation(rsg[k][0:P, :Wb], bank(bs, 0, P, Wb), AF.Ln, bias=EPS, scale=1.0 / dh), [PB[bs]], [rsg[k]])
                V("act", lambda e: e.activation(rsg[k][0:P, :Wb], rsg[k][0:P, :Wb], AF.Exp, scale=-0.5), [rsg[k]], [rsg[k]])
                o = nog()
                if rope is None:
                    V("dve", lambda e: e.scalar_tensor_tensor(o[0:P, :Wb], po, der[0:P, gcol:gcol + 1], rsg[k][0:P, :Wb], ALU.mult, ALU.mult),
                      [PB[bo], der, rsg[k]], [o])
                else:
                    pr = bank(br, 0, P, Wb)
                    V("dve", lambda e: e.scalar_tensor_tensor(ta[k][0:P, :Wb], po, der[0:P, gcol:gcol + 1], rsg[k][0:P, :Wb], ALU.mult, ALU.mult),
                      [PB[bo], der, rsg[k]], [ta[k]])
                    V("dve", lambda e: e.scalar_tensor_tensor(tb[k][0:P, :Wb], pr, der[0:P, gpcol:gpcol + 1], rsg[k][0:P, :Wb], ALU.mult, ALU.mult),
                      [PB[br], der, rsg[k]], [tb[k]])
                    V("dve", lambda e: e.tensor_tensor(ta[k][0:P, :Wb], ta[k][0:P, :Wb], rope[0:P, 0, :Wb], ALU.mult), [ta[k], rbuf], [ta[k]])
                    V("dve", lambda e: e.tensor_tensor(tb[k][0:P, :Wb], tb[k][0:P, :Wb], rope[0:P, 1, :Wb], ALU.mult), [tb[k], rbuf], [tb[k]])
                    V("dve", lambda e: e.tensor_tensor(o[0:P, :Wb], ta[k][0:P, :Wb], tb[k][0:P, :Wb], ALU.add), [ta[k], tb[k]], [o])
                store(name, o, P, c0, Wb)
            pend.append(stage2)
            flush(keep=1)

        def plaingrp(name, func, scale, c0, Wb):
            bo = nb()
            proj(bo, FMOFF[name][0], 128, Wb)
            o = nog()
            V("act", lambda e: e.activation(o[:, :Wb], bank(bo, 0, 128, Wb), func, scale=scale), [PB[bo]], [o])
            store(name, o, 128, c0, Wb)

        rope_buf = [None]
        import os
        CUT = int(os.environ.get("P1CUT", "99"))
        for blk_i in range(9 if CUT >= 99 else 1):
            c0 = blk_i * 512
            Wb = 512 if blk_i < 8 else 256
            s = 0 if blk_i < 8 else 1
            x_t = xT[blk_i % 2]
            kb.dma("sp", x_t[:, :, :Wb], xsrc[:, c0:c0 + Wb].rearrange("(c p) t -> p c t", p=128), reads=[xsrc], writes=[x_t])
            ra, rb = rA[blk_i % 2], rB[blk_i % 2]
            if blk_i < 8:
                kb.dma("sp", ra[:], ropeA_d[:, :, c0:c0 + 512].rearrange("a p t -> p a t"), reads=[ropeA_d], writes=[ra])
                kb.dma("sp", rb[:], ropeB_d[:, :, c0:c0 + 512].rearrange("a p t -> p a t"), reads=[ropeB_d], writes=[rb])
            if CUT < 2:
                continue
            for c in range(8):
                V("act", lambda e, c=c: e.activation(sq[:, c, :Wb], x_t[:, c, :Wb], AF.Square), [x_t], [sq])
            b0 = nb()

            def ssm(e):
                ins = None
                for c in range(8):
                    ins = e.matmul(bank(b0, 0, 128, Wb), ones_bf, sq[:, c, :Wb], start=(c == 0), stop=(c == 7))
                return ins
            V("pe", ssm, [sq, cb], [PB[b0]])
            V("act", lambda e: e.activation(rs[:, :Wb], bank(b0, 0, 128, Wb), AF.Ln, bias=EPS, scale=1.0 / D), [PB[b0]], [rs])
            V("act", lambda e: e.activation(rs[:, :Wb], rs[:, :Wb], AF.Exp, scale=-0.5), [rs], [rs])
            for c in range(8):
                t_ = tmp[c % 2]
                V("dve", lambda e, c=c, t_=t_: e.scalar_tensor_tensor(t_[:, :Wb], x_t[:, c, :Wb], gm[:, 0, c, s:s + 1], rs[:, :Wb], ALU.mult, ALU.mult),
                  [x_t, gm, rs], [t_])
                V("act", lambda e, c=c, t_=t_: e.activation(hT[:, c, :Wb], t_[:, :Wb], AF.Identity, bias=modT[:, c, s:s + 1], scale=1.0),
                  [t_, modT], [hT])
            isl = blk_i < 8
            rope_buf[0] = ra
            if CUT < 3:
                continue
            normgrp("Aq0", 128, 64, C_AQ, c0, Wb, ra if isl else None, C_AQP)
            if CUT < 4:
                continue
            normgrp("Aq1", 128, 64, C_AQ, c0, Wb, ra if isl else None, C_AQP)
            normgrp("Ak", 128, 64, C_AK, c0, Wb, ra if isl else None, C_AKP)
            rope_buf[0] = rb
            for h in range(4):
                normgrp(f"Bq{h}", 64, 32, C_BQ, c0, Wb, rb if isl else None, C_BQP)
            for h in range(4):
                normgrp(f"Bk{h}", 64, 32, C_BK, c0, Wb, rb if isl else None, C_BKP)
            if CUT < 5:
                continue
            flush()
            plaingrp("Cq0", AF.Identity, 1.0, c0, Wb)
            plaingrp("Cq1", AF.Identity, 1.0, c0, Wb)
            plaingrp("Ck0", AF.Identity, 0.125, c0, Wb)
            plaingrp("Ck1", AF.Identity, 0.125, c0, Wb)
            plaingrp("Cg0", AF.Silu, 1.0, c0, Wb)
            plaingrp("Cg1", AF.Silu, 1.0, c0, Wb)
            if CUT < 6:
                continue
            normgrp("Dq0", 128, 64, C_DQ, c0, Wb)
            normgrp("Dq1", 128, 64, C_DQ, c0, Wb)
            normgrp("Dk0", 128, 64, C_DK, c0, Wb)
            normgrp("Dk1", 128, 64, C_DK, c0, Wb)
            flush()
            if CUT < 7:
                continue
            for tt in range(Wb // 128):
                tok0 = c0 + tt * 128
                k = tt % 2
                b1, b2, b3 = nb(), nb(), nb()

                def tmm(e, tt=tt, b1=b1, b2=b2, b3=b3):
                    ins = None
                    for bi, cc, w in ((b1, TM0, 512), (b2, TM0 + 512, 512), (b3, TM0 + 1024, 128)):
                        for kc in range(8):
                            ins = e.matmul(bank(bi, 0, 128, w), hT[:, kc, tt * 128:(tt + 1) * 128], win[:, kc, cc:cc + w],
                                           start=(kc == 0), stop=(kc == 7))
                    return ins
                V("pe", tmm, [hT, win], [PB[b1], PB[b2], PB[b3]])
                p1 = bank(b1)
                p2 = bank(b2)
                p3 = bank(b3, 0, 128, 128)
                V("act", lambda e, k=k, p2=p2: e.copy(VCs[k][:], p2[:, 0:256]), [PB[b2]], [VCs[k]])
                if CUT < 8:
                    kb.dma("pool", VC_d[tok0:tok0 + 128], VCs[k][:], reads=[VCs[k]], writes=[])
                    continue
                V("act", lambda e, k=k, p1=p1: e.copy(VBs[k][:, :, 0:64], p1[:, 0:256].rearrange("p (h d) -> p h d", h=4)), [PB[b1]], [VBs[k]])
                V("act", lambda e, k=k, p1=p1: e.copy(VDs[k][:, :, 0:64], p1[:, 256:512].rearrange("p (h d) -> p h d", h=4)), [PB[b1]], [VDs[k]])
                if CUT < 9:
                    for dst, srcb in ((VB_d, VBs), (VD_d, VDs)):
                        kb.dma("pool", dst[tok0:tok0 + 128], srcb[k][:], reads=[srcb[k]], writes=[])
                    continue
                SK = os.environ.get("P1SK", "")
                for hh in range(4):
                    if "k" in SK:
                        break
                    V("dve", lambda e, k=k, p2=p2, hh=hh: e.tensor_scalar(
                        KFs[k][:, hh * 64:(hh + 1) * 64], p2[:, 256 + hh * 64:256 + (hh + 1) * 64], kdec[:, hh:hh + 1], None, ALU.mult),
                      [PB[b2], kdec], [KFs[k]])
                    V("dve", lambda e, k=k, p2=p2, hh=hh: e.tensor_scalar(
                        KBs[k][:, hh * 64:(hh + 1) * 64], p2[:, 256 + hh * 64:256 + (hh + 1) * 64], kdec[:, 4 + hh:5 + hh], None, ALU.mult),
                      [PB[b2], kdec], [KBs[k]])
                if "a" not in SK:
                  V("act", lambda e, k=k, p3=p3: e.copy(VAs[k][:, :, 0:64], p3.rearrange("p (h d) -> p h d", h=2)), [PB[b3]], [VAs[k]])
                for dst, srcb in ((VB_d, VBs), (VD_d, VDs), (VA_d, VAs), (VC_d, VCs), (KF_d, KFs), (KBk_d, KBs)):
                    if "s" in SK and dst in (VA_d, KF_d, KBk_d):
                        continue
                    kb.dma("pool", dst[tok0:tok0 + 128], srcb[k][:], reads=[srcb[k]], writes=[])

    def attn_generic(st, tag, nheads_q, load_head, qtile_list, chunk_fn, qw, epilogue, nS):
        pass

    def phase2A(l, st, need_ctx):
        QT2 = [kb.sbuf(st, f"aQT{i}", [128, NK], BF16) for i in range(2)]
        KT2 = [kb.sbuf(st, f"aKT{i}", [128, NK], BF16) for i in range(2)]
        for tl in QT2 + KT2:
            V("pool", lambda e, tl=tl: e.memset(tl[64:128, :], 0.0), [], [tl])
        Vv2 = [kb.sbuf(st, f"aV{i}", [128, 34, 128], BF16) for i in range(2)]
        PT = [kb.sbuf(st, f"aPT{i}", [128, 640], BF16) for i in range(3)]
        tq = [kb.sbuf(st, f"atq{i}", [128, 128], F32) for i in range(2)]
        stg = [kb.sbuf(st, f"astg{i}", [64, 512], BF16) for i in range(2)]
        ntile = 34 if need_ctx else 32
        for h in range(4):
            g = h // 2
            QT, KT, Vv = QT2[h % 2], KT2[h % 2], Vv2[h % 2]
            src = fm[f"Aq{h // 2}"]
            kb.dma("sp", QT[0:64, :], src[(h % 2) * 64:(h % 2) * 64 + 64, :], reads=[], writes=[QT])
            kb.dma("sp", KT[0:64, :], fm["Ak"][g * 64:(g + 1) * 64, :], reads=[], writes=[KT])
            kb.dma("sp", Vv[:], VA_d[:, g, :].rearrange("(t p) d -> p t d", p=128), reads=[], writes=[Vv])

            def chunks(i):
                if i >= 32:
                    return [(32, None), (33, None)]
                r = []
                if i > 0:
                    r.append((i - 1, "prev"))
                r.append((i, None))
                if i < 31:
                    r.append((i + 1, "next"))
                return r + [(32, None), (33, None)]

            def s_stage(i):
                ch = chunks(i)
                sb = (i % 3) * 2
                pt = PT[i % 3]

                def f(e):
                    ins = None
                    for ci, (kc, m) in enumerate(ch):
                        o_ = pst[:, sb * 512 + ci * 128: sb * 512 + (ci + 1) * 128]
                        ins = e.matmul(o_, KT[:, kc * 128:(kc + 1) * 128], QT[:, i * 128:(i + 1) * 128], start=True, stop=(m is None))
                        if m is not None:
                            ins = e.matmul(o_, ident, amask[m], start=False, stop=True)
                    return ins
                V("pe", f, [KT, QT, cb], [PB[sb], PB[sb + 1]])
                n = len(ch) * 128
                V("act", lambda e: e.activation(pt[:, :n], pst[:, sb * 512: sb * 512 + n], AF.Exp), [PB[sb], PB[sb + 1]], [pt])

            def o_stage(i):
                ch = chunks(i)
                pt = PT[i % 3]
                ob = 6 + (i % 2)

                def f(e):
                    ins = None
                    for ci, (kc, m) in enumerate(ch):
                        ins = e.matmul(bank(ob, 0, 128, 128), Vv[:, kc, :], pt[:, ci * 128:(ci + 1) * 128], start=(ci == 0), stop=(ci == len(ch) - 1))
                    return ins
                V("pe", f, [Vv, pt], [PB[ob]])
                t_ = tq[i % 2]
                sg = stg[(i // 4) % 2]
                V("act", lambda e: e.activation(t_[64:128, :], bank(ob, 64, 128, 128), AF.Ln, bias=der[64:128, C_ESINK + h:C_ESINK + h + 1], scale=1.0),
                  [PB[ob], der], [t_])
                V("act", lambda e: e.activation(t_[64:128, :], t_[64:128, :], AF.Exp, scale=-1.0), [t_], [t_])
                V("dve", lambda e: e.tensor_tensor(sg[:, (i % 4) * 128:(i % 4 + 1) * 128], bank(ob, 0, 64, 128), t_[64:128, :], ALU.mult),
                  [PB[ob], t_], [sg])
                if i % 4 == 3 or i == ntile - 1:
                    c0 = (i // 4) * 512
                    wd = (i % 4 + 1) * 128
                    kb.dma("pool", attnT[h * 64:(h + 1) * 64, c0:c0 + wd], sg[:, :wd], reads=[sg], writes=[])

            for i in range(ntile + 2):
                if i < ntile:
                    s_stage(i)
                if i >= 2:
                    o_stage(i - 2)

    def phase2D(l, st, need_ctx):
        QT2 = [kb.sbuf(st, f"dQT{i}", [128, NK], BF16) for i in range(2)]
        KT2 = [kb.sbuf(st, f"dKT{i}", [128, NK], BF16) for i in range(2)]
        for tl in QT2 + KT2:
            V("pool", lambda e, tl=tl: e.memset(tl[64:128, :], 0.0), [], [tl])
        Vv2 = [kb.sbuf(st, f"dV{i}", [128, 34, 128], BF16) for i in range(2)]
        tabf2 = [kb.sbuf(st, f"dtabf{i}", [128, 12, 128], F32) for i in range(2)]
        mskf = kb.sbuf(st, "dmskf", [128, 12, 128], F32)
        tab2 = [kb.sbuf(st, f"dtab{i}", [128, 12, 128], BF16) for i in range(2)]
        PT = [kb.sbuf(st, f"dPT{i}", [128, 896], BF16) for i in range(3)]
        tq = [kb.sbuf(st, f"dtq{i}", [64, 128], F32) for i in range(2)]
        stg = [kb.sbuf(st, f"dstg{i}", [64, 512], BF16) for i in range(2)]
        kb.dma("sp", mskf[:], dmask_d[:], reads=[], writes=[mskf])
        ntile = 34 if need_ctx else 32
        for h in range(4):
            QT, KT, Vv, tabf, tab = QT2[h % 2], KT2[h % 2], Vv2[h % 2], tabf2[h % 2], tab2[h % 2]
            src = fm[f"Dq{h // 2}"]
            kb.dma("sp", QT[0:64, :], src[(h % 2) * 64:(h % 2) * 64 + 64, :], reads=[], writes=[QT])
            kb.dma("sp", KT[0:64, :], fm[f"Dk{h // 2}"][(h % 2) * 64:(h % 2) * 64 + 64, :], reads=[], writes=[KT])
            kb.dma("sp", Vv[:], VD_d[:, h, :].rearrange("(t p) d -> p t d", p=128), reads=[], writes=[Vv])
            kb.dma("sp", tabf[:], dbias_d[l, h], reads=[], writes=[tabf])
            V("dve", lambda e: e.tensor_tensor(tab[:], tabf[:], mskf[:], ALU.add), [tabf, mskf], [tab])

            def chunks(a):
                if a >= 32:
                    return [(32, None), (33, None)]
                if 2 <= a <= 29:
                    r = [(a + dl, dl + 2) for dl in range(-2, 3)]
                else:
                    ks = range(0, 4) if a < 2 else range(28, 32)
                    r = [(kc, 5 + (kc - a) + 3) for kc in ks]
                return r + [(32, None), (33, None)]

            def s_stage(i):
                ch = chunks(i)
                sb = (i % 3) * 2
                pt = PT[i % 3]

                def f(e):
                    ins = None
                    for ci, (kc, m) in enumerate(ch):
                        o_ = pst[:, sb * 512 + ci * 128: sb * 512 + (ci + 1) * 128]
                        ins = e.matmul(o_, KT[:, kc * 128:(kc + 1) * 128], QT[:, i * 128:(i + 1) * 128], start=True, stop=(m is None))
                        if m is not None:
                            ins = e.matmul(o_, ident, tab[:, m, :], start=False, stop=True)
                    return ins
                V("pe", f, [KT, QT, cb, tab], [PB[sb], PB[sb + 1]])
                n = len(ch) * 128
                V("act", lambda e: e.activation(pt[:, :n], pst[:, sb * 512: sb * 512 + n], AF.Exp), [PB[sb], PB[sb + 1]], [pt])

            def o_stage(i):
                ch = chunks(i)
                pt = PT[i % 3]
                ob = 6 + (i % 2)

                def f(e):
                    ins = None
                    for ci, (kc, m) in enumerate(ch):
                        ins = e.matmul(bank(ob, 0, 128, 128), Vv[:, kc, :], pt[:, ci * 128:(ci + 1) * 128], start=(ci == 0), stop=(ci == len(ch) - 1))
                    return ins
                V("pe", f, [Vv, pt], [PB[ob]])
                t_ = tq[i % 2]
                sg = stg[(i // 4) % 2]
                V("dve", lambda e: e.reciprocal(t_[:], bank(ob, 64, 128, 128)), [PB[ob]], [t_])
                V("dve", lambda e: e.tensor_tensor(sg[:, (i % 4) * 128:(i % 4 + 1) * 128], bank(ob, 0, 64, 128), t_[:], ALU.mult),
                  [PB[ob], t_], [sg])
                if i % 4 == 3 or i == ntile - 1:
                    c0 = (i // 4) * 512
                    wd = (i % 4 + 1) * 128
                    kb.dma("pool", attnT[768 + h * 64:768 + (h + 1) * 64, c0:c0 + wd], sg[:, :wd], reads=[sg], writes=[])

            for i in range(ntile + 2):
                if i < ntile:
                    s_stage(i)
                if i >= 2:
                    o_stage(i - 2)

    def phase2B(l, st, need_ctx):
        KTz4 = [kb.sbuf(st, f"bKTz{i}", [128, NK], BF16) for i in range(4)]
        KTz = KTz4
        Vv2 = [kb.sbuf(st, f"bV{i}", [128, 34, 128], BF16) for i in range(2)]
        QTs = [kb.sbuf(st, f"bQT{i}", [128, 512], BF16) for i in range(2)]
        for tl in KTz + QTs:
            V("pool", lambda e, tl=tl: e.memset(tl[:], 0.0), [], [tl])
        PT = [kb.sbuf(st, f"bPT{i}", [128, 2, 512], BF16) for i in range(4)]
        Oc = [[kb.sbuf(st, f"bOc{i}{j}", [128, 512], F32) for j in range(2)] for i in range(2)]
        r1 = kb.sbuf(st, "br1", [64, 512], F32)
        t1 = kb.sbuf(st, "bt1", [64, 512], F32)
        t2 = kb.sbuf(st, "bt2", [64, 512], F32)
        osq = kb.sbuf(st, "bosq", [64, 512], BF16)
        sg = [kb.sbuf(st, f"bsg{i}", [64, 512], BF16) for i in range(2)]
        nqb = 9 if need_ctx else 8
        gstep = [0]
        pend_epi = []
        wstg = [kb.sbuf(st, f"wstg{i}", [128, 4096], F32) for i in range(2)]
        wstb = [kb.sbuf(st, f"wstb{i}", [128, 4096], BF16) for i in range(2)]
        jobs = [("wo", i) for i in range(2)] + [("w1", i) for i in range(8)] + [("w2", i) for i in range(8)]
        jn = [0]

        def precast_job():
            if jn[0] >= len(jobs):
                return
            n_ = jn[0]
            jn[0] += 1
            kind, i = jobs[n_]
            a, b = wstg[n_ % 2], wstb[n_ % 2]
            if kind == "wo":
                src = w_out_d[l, :, 4 * i:4 * i + 4, :]
                dst = wo_bf[:, 4 * i:4 * i + 4, :]
                bsrc = b[:].rearrange("p (a b) -> p a b", a=4)
            elif kind == "w1":
                src = w1_d[l, :, i:i + 1, :]
                dst = w1_bf[:, :, i, :].rearrange("f p d -> p f d")
                bsrc = b[:].rearrange("p (f d) -> p f d", f=32)
            else:
                src = w2_d[l, :, 4 * i:4 * i + 4, :]
                dst = None
            kb.dma("sp", a[:].rearrange("p (a b) -> p a b", a=src.shape[1]), src, reads=[], writes=[a])
            V("pool", lambda e: e.tensor_copy(b[:], a[:]), [a], [b])
            if dst is not None:
                kb.dma("pool", dst, bsrc, reads=[b], writes=[])
            else:
                bv = b[:].rearrange("p (f c d) -> p f c d", f=4, c=8)
                for ff in range(4):
                    kb.dma("pool", w2_bf[:, :, 4 * i + ff, :].rearrange("c p d -> p c d"), bv[:, ff], reads=[b], writes=[])
        for h in range(4):
            KTz = KTz4[2 * (h % 2):2 * (h % 2) + 2]
            Vv = Vv2[h % 2]
            kb.dma("sp", KTz[0][0:32, :], fm[f"Bk{h}"][0:32, :], reads=[], writes=[KTz[0]])
            kb.dma("sp", KTz[1][32:64, :], fm[f"Bk{h}"][32:64, :], reads=[], writes=[KTz[1]])
            kb.dma("sp", Vv[:], VB_d[:, h, :].rearrange("(t p) d -> p t d", p=128), reads=[], writes=[Vv])
            for qb in range(nqb):
                Wq = 512 if qb < 8 else 256
                q0 = qb * 512
                itn = h * nqb + qb
                QT = QTs[itn % 2]
                kb.dma("sp", QT[0:64, :Wq], fm[f"Bq{h}"][:, q0:q0 + Wq], reads=[], writes=[QT])
                chs = list(range(34)) if qb < 8 else [32, 33]
                n = len(chs)
                ob = [6, 7]
                slot = {}

                LAGB = 3

                def stage(it):
                    sreads, swrites = [], []
                    do_s = it < n
                    do_o = LAGB <= it < n + LAGB
                    if not do_s and not do_o:
                        return
                    if do_s:
                        kc = chs[it]
                        gs = gstep[0]
                        gstep[0] += 1
                        sb = 2 * (gs % 3)
                        pt = PT[gs % 4]
                        slot[it] = pt
                        sreads += [KTz[0], KTz[1], QT]
                        swrites += [PB[sb], PB[sb + 1]]
                    if do_o:
                        io = it - LAGB
                        kco = chs[io]
                        pto = slot.pop(io)
                        sreads += [Vv, pto]
                        swrites += [PB[ob[0]], PB[ob[1]]]

                    def f(e):
                        ins = None
                        if do_s:
                            e.matmul(bank(sb, 0, 128, Wq), KTz[0][:, kc * 128:(kc + 1) * 128], QT[:, :Wq], start=True, stop=True)
                            ins = e.matmul(bank(sb + 1, 0, 128, Wq), KTz[1][:, kc * 128:(kc + 1) * 128], QT[:, :Wq], start=True, stop=True)
                        if do_o:
                            e.matmul(bank(ob[0], 0, 128, Wq), Vv[:, kco, :], pto[:, 0, :Wq], start=(io == 0), stop=(io == n - 1))
                            ins = e.matmul(bank(ob[1], 0, 128, Wq), Vv[:, kco, :], pto[:, 1, :Wq], start=(io == 0), stop=(io == n - 1))
                        return ins
                    V("pe", f, sreads, swrites)
                    if do_s:
                        src = pst[:, sb * 512:(sb + 2) * 512].rearrange("p (b w) -> p b w", b=2)[:, :, :Wq]
                        V("act", lambda e: e.activation(pt[:, :, :Wq], src, AF.Exp), [PB[sb], PB[sb + 1]], [pt])

                for it in range(n + LAGB):
                    stage(it)
                    if it == min(6, n) and pend_epi:
                        pend_epi.pop(0)()

                oc = Oc[itn % 2]
                V("dve", lambda e: e.tensor_copy(oc[0][:, :Wq], bank(ob[0], 0, 128, Wq)), [PB[ob[0]]], [oc[0]])
                V("dve", lambda e: e.tensor_copy(oc[1][:, :Wq], bank(ob[1], 0, 128, Wq)), [PB[ob[1]]], [oc[1]])

                def epilogue(h=h, qb=qb, Wq=Wq, q0=q0, oc=oc, itn=itn):
                    c1, c2 = oc
                    V("dve", lambda e: e.reciprocal(r1[:, :Wq], c1[64:128, :Wq]), [c1], [r1])
                    V("dve", lambda e: e.tensor_tensor(t1[:, :Wq], c1[0:64, :Wq], r1[:, :Wq], ALU.mult), [c1, r1], [t1])
                    V("dve", lambda e: e.reciprocal(r1[:, :Wq], c2[64:128, :Wq]), [c2], [r1])
                    V("dve", lambda e: e.tensor_tensor(t2[:, :Wq], c2[0:64, :Wq], r1[:, :Wq], ALU.mult), [c2, r1], [t2])
                    V("dve", lambda e: e.scalar_tensor_tensor(t1[:, :Wq], t2[:, :Wq], der[0:64, C_NLAM:C_NLAM + 1], t1[:, :Wq], ALU.mult, ALU.add),
                      [t1, t2, der], [t1])
                    V("act", lambda e: e.activation(osq[:, :Wq], t1[:, :Wq], AF.Square), [t1], [osq])
                    gs = gstep[0]
                    gstep[0] += 1
                    ssb = 2 * (gs % 3)
                    V("pe", lambda e: e.matmul(bank(ssb, 0, 64, Wq), blk64[0:64, 0:64], osq[:, :Wq], start=True, stop=True), [osq, cb], [PB[ssb]])
                    V("act", lambda e: e.activation(t2[:, :Wq], bank(ssb, 0, 64, Wq), AF.Ln, bias=EPS, scale=1.0 / 64), [PB[ssb]], [t2])
                    V("act", lambda e: e.activation(t2[:, :Wq], t2[:, :Wq], AF.Exp, scale=-0.5), [t2], [t2])
                    s_ = sg[itn % 2]
                    V("dve", lambda e: e.scalar_tensor_tensor(s_[:, :Wq], t1[:, :Wq], der[0:64, C_BSUB:C_BSUB + 1], t2[:, :Wq], ALU.mult, ALU.mult),
                      [t1, t2, der], [s_])
                    kb.dma("pool", attnT[256 + h * 64:256 + (h + 1) * 64, q0:q0 + Wq], s_[:, :Wq], reads=[s_], writes=[])
                pend_epi.append(epilogue)
                precast_job()
        while pend_epi:
            pend_epi.pop(0)()
        while jn[0] < len(jobs):
            precast_job()

    def phase2C(l, st, need_ctx):
        Vv = kb.sbuf(st, "cV", [128, 34, 256], BF16)
        KF = kb.sbuf(st, "cKF", [128, 34, 256], BF16)
        KBw = kb.sbuf(st, "cKB", [128, 34, 256], BF16)
        kb.dma("sp", Vv[:], VC_d[:, :].rearrange("(t p) d -> p t d", p=128), reads=[], writes=[Vv])
        kb.dma("sp", KF[:], KF_d[:, :].rearrange("(t p) d -> p t d", p=128), reads=[], writes=[KF])
        kb.dma("sp", KBw[:], KBk_d[:, :].rearrange("(t p) d -> p t d", p=128), reads=[], writes=[KBw])
        S = kb.sbuf(st, "cS", [64, 8, 64], F32)
        Sall = kb.sbuf(st, "cSall", [128, 34, 8, 64], BF16)
        V("pool", lambda e: e.memset(Sall[64:128], 0.0), [], [Sall])
        V("dve", lambda e: e.memset(S[:], 0.0), [], [S])
        order_f = [32, 33] + list(range(32))
        order_b = [33, 32] + list(range(31, -1, -1))
        for s_i in range(34):
            cf, cbk = order_f[s_i], order_b[s_i]
            V("act", lambda e, cf=cf: e.copy(Sall[0:64, cf, 0:4, :], S[:, 0:4, :]), [S], [Sall])
            V("act", lambda e, cbk=cbk: e.copy(Sall[0:64, cbk, 4:8, :], S[:, 4:8, :]), [S], [Sall])
            if s_i == 33:
                break
            bk = s_i % 2

            def f(e, cf=cf, cbk=cbk, bk=bk):
                ins = None
                for h in range(4):
                    ins = e.matmul(pst[0:64, bk * 512 + h * 64: bk * 512 + (h + 1) * 64], KF[:, cf, h * 64:(h + 1) * 64], Vv[:, cf, h * 64:(h + 1) * 64],
                                   start=True, stop=True)
                    ins = e.matmul(pst[0:64, bk * 512 + (4 + h) * 64: bk * 512 + (5 + h) * 64], KBw[:, cbk, h * 64:(h + 1) * 64],
                                   Vv[:, cbk, h * 64:(h + 1) * 64], start=True, stop=True)
                return ins
            V("pe", f, [KF, KBw, Vv], [PB[bk]])
            V("dve", lambda e: e.tensor_tensor(S[:], S[:], G128[:], ALU.mult), [S, G128], [S])
            V("dve", lambda e, bk=bk: e.tensor_tensor(S[:], S[:], bank(bk, 0, 64, 512).rearrange("p (k d) -> p k d", k=8), ALU.add), [S, PB[bk]], [S])
        QT2 = [kb.sbuf(st, f"cQT{i}", [128, NK], BF16) for i in range(2)]
        KT2 = [kb.sbuf(st, f"cKT{i}", [128, NK], BF16) for i in range(2)]
        for tl in QT2 + KT2:
            V("pool", lambda e, tl=tl: e.memset(tl[64:128, :], 0.0), [], [tl])
        GTs = [kb.sbuf(st, f"cGT{i}", [64, 512], BF16) for i in range(2)]
        inF = [kb.sbuf(st, f"cinF{i}", [128, 128], BF16) for i in range(3)]
        inB = [kb.sbuf(st, f"cinB{i}", [128, 128], BF16) for i in range(3)]
        qdF2 = [kb.sbuf(st, f"cqdF{i}", [128, NK], BF16) for i in range(2)]
        qdB2 = [kb.sbuf(st, f"cqdB{i}", [128, NK], BF16) for i in range(2)]
        for tl in qdF2 + qdB2:
            V("pool", lambda e, tl=tl: e.memset(tl[64:128, :], 0.0), [], [tl])
        osq = kb.sbuf(st, "cosq", [64, 512], BF16)
        rr = kb.sbuf(st, "crr", [64, 512], F32)
        tt_ = kb.sbuf(st, "ctt", [64, 512], F32)
        sg = [kb.sbuf(st, f"csg{i}", [64, 512], BF16) for i in range(2)]
        nqb = 9 if need_ctx else 8
        pend_c = []
        for h in range(4):
            QT, KT = QT2[h % 2], KT2[h % 2]
            kb.dma("sp", QT[0:64, :], fm[f"Cq{h // 2}"][(h % 2) * 64:(h % 2) * 64 + 64, :], reads=[], writes=[QT])
            kb.dma("sp", KT[0:64, :], fm[f"Ck{h // 2}"][(h % 2) * 64:(h % 2) * 64 + 64, :], reads=[], writes=[KT])
            qdFh, qdBh = qdF2[h % 2], qdB2[h % 2]
            qv = QT[0:64, :].rearrange("p (c i) -> p c i", i=128)
            V("dve", lambda e: e.tensor_tensor(qdFh[0:64, :].rearrange("p (c i) -> p c i", i=128), qv,
                                               DQ[:, h, :].unsqueeze(1).to_broadcast([64, 34, 128]), ALU.mult), [QT, DQ], [qdFh])
            V("dve", lambda e: e.tensor_tensor(qdBh[0:64, :].rearrange("p (c i) -> p c i", i=128), qv,
                                               DQ[:, 4 + h, :].unsqueeze(1).to_broadcast([64, 34, 128]), ALU.mult), [QT, DQ], [qdBh])
            for qb in range(nqb):
                Wq = 512 if qb < 8 else 256
                q0 = qb * 512
                itn = h * nqb + qb
                GT = GTs[itn % 2]
                kb.dma("sp", GT[:, :Wq], fm[f"Cg{h // 2}"][(h % 2) * 64:(h % 2) * 64 + 64, q0:q0 + Wq], reads=[], writes=[GT])
                ob = 4 + (itn % 2)
                nt = Wq // 128
                def c_stage1(tt):
                    n = qb * 4 + tt
                    r_ = n % 3
                    sbk = r_
                    cs = slice(n * 128, (n + 1) * 128)
                    V("pe", lambda e: e.matmul(bank(sbk, 0, 128, 128), KT[:, cs], QT[:, cs], start=True, stop=True), [KT, QT], [PB[sbk]])
                    V("dve", lambda e: e.tensor_tensor(inF[r_][:], bank(sbk, 0, 128, 128), DT[:, h, :], ALU.mult), [PB[sbk], DT], [inF[r_]])
                    V("dve", lambda e: e.tensor_tensor(inB[r_][:], bank(sbk, 0, 128, 128), DT[:, 4 + h, :], ALU.mult), [PB[sbk], DT], [inB[r_]])

                def c_stage2(tt):
                    n = qb * 4 + tt
                    r_ = n % 3

                    cs = slice(n * 128, (n + 1) * 128)

                    def f(e):
                        o_ = pst[0:64, ob * 512 + tt * 128: ob * 512 + (tt + 1) * 128]
                        e.matmul(o_, Vv[:, n, h * 64:(h + 1) * 64], inF[r_][:], start=True, stop=False)
                        e.matmul(o_, Sall[:, n, h, :], qdFh[:, cs], start=False, stop=False)
                        e.matmul(o_, Vv[:, n, h * 64:(h + 1) * 64], inB[r_][:], start=False, stop=False)
                        return e.matmul(o_, Sall[:, n, 4 + h, :], qdBh[:, cs], start=False, stop=True)
                    V("pe", f, [Vv, Sall, inF[r_], inB[r_], qdFh, qdBh], [PB[ob]])

                for tt in range(nt + 1):
                    if tt < nt:
                        c_stage1(tt)
                    if tt >= 1:
                        c_stage2(tt - 1)
                    if tt == min(2, nt - 1) and pend_c:
                        pend_c.pop(0)()
                def epilogue(h=h, qb=qb, Wq=Wq, q0=q0, ob=ob, GT=GT, itn=itn):
                    po = bank(ob, 0, 64, Wq)
                    V("act", lambda e: e.activation(osq[:, :Wq], po, AF.Square), [PB[ob]], [osq])
                    V("pe", lambda e: e.matmul(bank(7, 0, 64, Wq), blk64[0:64, 0:64], osq[:, :Wq], start=True, stop=True), [osq, cb], [PB[7]])
                    V("act", lambda e: e.activation(rr[:, :Wq], bank(7, 0, 64, Wq), AF.Ln, bias=EPS, scale=1.0 / 64), [PB[7]], [rr])
                    V("act", lambda e: e.activation(rr[:, :Wq], rr[:, :Wq], AF.Exp, scale=-0.5), [rr], [rr])
                    V("dve", lambda e: e.scalar_tensor_tensor(tt_[:, :Wq], po, der[0:64, C_CGN:C_CGN + 1], rr[:, :Wq], ALU.mult, ALU.mult),
                      [PB[ob], der, rr], [tt_])
                    s_ = sg[itn % 2]
                    V("dve", lambda e: e.tensor_tensor(s_[:, :Wq], tt_[:, :Wq], GT[:, :Wq], ALU.mult), [tt_, GT], [s_])
                    kb.dma("pool", attnT[512 + h * 64:512 + (h + 1) * 64, q0:q0 + Wq], s_[:, :Wq], reads=[s_], writes=[])
                pend_c.append(epilogue)
        while pend_c:
            pend_c.pop(0)()

    def phase3(l, st, xsrc, xdst, need_ctx):
        wo = kb.sbuf(st, "wo", [128, 8, D], BF16)
        kb.dma("sp", wo[:], wo_bf[:], reads=[], writes=[wo])
        w1c = [kb.sbuf(st, f"w1c{i}", [128, 8, 128], BF16) for i in range(3)]
        w2c = [kb.sbuf(st, f"w2c{i}", [128, 32, 128], BF16) for i in range(2)]
        xTs = [kb.sbuf(st, f"x3{i}", [128, 8, 512], F32) for i in range(2)]
        aTs = [kb.sbuf(st, f"a3{i}", [128, 8, 512], BF16) for i in range(2)]
        hTs = [kb.sbuf(st, f"h3{i}", [128, 8, 512], BF16) for i in range(2)]
        sq = kb.sbuf(st, "sq3", [128, 8, 512], BF16)
        uT = kb.sbuf(st, "u3", [128, 32, 512], BF16)
        rs = kb.sbuf(st, "rs3", [128, 512], F32)
        tmp = [kb.sbuf(st, f"t3{i}", [128, 512], F32) for i in range(2)]
        xo = [kb.sbuf(st, f"xo{i}", [128, 512], F32) for i in range(2)]
        nblk = 9 if need_ctx else 8
        bc = [0]
        wcnt = [0]

        def nb():
            bc[0] = (bc[0] + 1) % 8
            return bc[0]

        def geom(blk_i):
            return blk_i * 512, (512 if blk_i < 8 else 256), (0 if blk_i < 8 else 1)

        def stageA(blk_i):
            c0, Wb, s = geom(blk_i)
            xT, aT, hT = xTs[blk_i % 2], aTs[blk_i % 2], hTs[blk_i % 2]
            kb.dma("sp", aT[:, :, :Wb], attnT[:, c0:c0 + Wb].rearrange("(c p) t -> p c t", p=128), reads=[], writes=[aT])
            kb.dma("sp", xT[:, :, :Wb], xsrc[:, c0:c0 + Wb].rearrange("(c p) t -> p c t", p=128), reads=[], writes=[xT])
            for c in range(8):
                b_ = nb()

                def f(e, c=c, b_=b_):
                    ins = None
                    for kc in range(8):
                        ins = e.matmul(bank(b_, 0, 128, Wb), wo[:, kc, c * 128:(c + 1) * 128], aT[:, kc, :Wb], start=(kc == 0), stop=(kc == 7))
                    return ins
                V("pe", f, [wo, aT], [PB[b_]])
                V("dve", lambda e, c=c, b_=b_: e.scalar_tensor_tensor(xT[:, c, :Wb], bank(b_, 0, 128, Wb), modT[:, 16 + c, s:s + 1], xT[:, c, :Wb],
                                                                     ALU.mult, ALU.add), [PB[b_], modT, xT], [xT])
                V("act", lambda e, c=c: e.activation(sq[:, c, :Wb], xT[:, c, :Wb], AF.Square), [xT], [sq])
            b0 = nb()

            def ssm(e):
                ins = None
                for c in range(8):
                    ins = e.matmul(bank(b0, 0, 128, Wb), ones_bf, sq[:, c, :Wb], start=(c == 0), stop=(c == 7))
                return ins
            V("pe", ssm, [sq, cb], [PB[b0]])
            V("act", lambda e: e.activation(rs[:, :Wb], bank(b0, 0, 128, Wb), AF.Ln, bias=EPS, scale=1.0 / D), [PB[b0]], [rs])
            V("act", lambda e: e.activation(rs[:, :Wb], rs[:, :Wb], AF.Exp, scale=-0.5), [rs], [rs])
            for c in range(8):
                t_ = tmp[c % 2]
                V("dve", lambda e, c=c, t_=t_: e.scalar_tensor_tensor(t_[:, :Wb], xT[:, c, :Wb], gm[:, 1, c, s:s + 1], rs[:, :Wb], ALU.mult, ALU.mult),
                  [xT, gm, rs], [t_])
                V("act", lambda e, c=c, t_=t_: e.activation(hT[:, c, :Wb], t_[:, :Wb], AF.Identity, bias=modT[:, 24 + c, s:s + 1], scale=1.0),
                  [t_, modT], [hT])

        def stageB(blk_i):
            c0, Wb, s = geom(blk_i)
            xT, hT = xTs[blk_i % 2], hTs[blk_i % 2]
            for fc in range(32):
                wc = w1c[wcnt[0] % 3]
                wcnt[0] += 1
                kb.dma("sp", wc[:], w1_bf[fc], reads=[], writes=[wc])
                b_ = nb()

                def f(e, wc=wc, b_=b_):
                    ins = None
                    for kc in range(8):
                        ins = e.matmul(bank(b_, 0, 128, Wb), wc[:, kc, :], hT[:, kc, :Wb], start=(kc == 0), stop=(kc == 7))
                    return ins
                V("pe", f, [wc, hT], [PB[b_]])
                t_ = tmp[fc % 2]
                V("act", lambda e, b_=b_, t_=t_: e.activation(t_[:, :Wb], bank(b_, 0, 128, Wb), AF.Relu), [PB[b_]], [t_])
                V("dve", lambda e, fc=fc, t_=t_: e.tensor_tensor(uT[:, fc, :Wb], t_[:, :Wb], t_[:, :Wb], ALU.mult), [t_], [uT])
            for c in range(8):
                wc = w2c[c % 2]
                kb.dma("sp", wc[:], w2_bf[c], reads=[], writes=[wc])
                b_ = nb()

                def f(e, wc=wc, b_=b_):
                    ins = None
                    for fc in range(32):
                        ins = e.matmul(bank(b_, 0, 128, Wb), wc[:, fc, :], uT[:, fc, :Wb], start=(fc == 0), stop=(fc == 31))
                    return ins
                V("pe", f, [wc, uT], [PB[b_]])
                o_ = xo[c % 2]
                V("dve", lambda e, c=c, b_=b_, o_=o_: e.scalar_tensor_tensor(o_[:, :Wb], bank(b_, 0, 128, Wb), modT[:, 40 + c, s:s + 1], xT[:, c, :Wb],
                                                                            ALU.mult, ALU.add), [PB[b_], modT, xT], [o_])
                if blk_i < 8 or xdst is xmid:
                    kb.dma("pool", xdst[c * 128:(c + 1) * 128, c0:c0 + Wb], o_[:, :Wb], reads=[o_], writes=[])

        stageA(0)
        for blk_i in range(nblk):
            if blk_i + 1 < nblk:
                stageA(blk_i + 1)
            stageB(blk_i)

    def run_phase(fn, *a):
        with ExitStack() as st:
            fn(*a[:1], st, *a[1:])
            kb.fence()

    for l in range(nlayers):
        need_ctx = l < DEPTH - 1
        xsrc = xin if l == 0 else xmid
        xdst = xmid if l < DEPTH - 1 else outT
        on = lambda nm: only is None or nm in only
        if on("p0"): run_phase(phase0, l)
        if debug and l == 0:
            dbg_mod = kb.dram("dbg_mod", [128, 96 + 64 + 16], F32, "ExternalOutput")
            kb.dma("sp", dbg_mod[:, 0:96], modT[:].rearrange("p a b -> p (a b)"), reads=[modT], writes=[])
            kb.dma("sp", dbg_mod[:, 96:160], der[:], reads=[der], writes=[])
            kb.dma("sp", dbg_mod[:, 160:168], kdec[:], reads=[kdec], writes=[])
            kb.fence()
        if on("p1"): run_phase(phase1, l, xsrc)
        if on("A"): run_phase(phase2A, l, need_ctx)
        if on("B"): run_phase(phase2B, l, need_ctx)
        if on("C"): run_phase(phase2C, l, need_ctx)
        if on("D"): run_phase(phase2D, l, need_ctx)
        if debug and l == 0:
            with ExitStack() as st:
                a = kb.sbuf(st, "dbga", [128, NK], BF16)
                b = kb.sbuf(st, "dbgb", [128, NK], F32)
                for c in range(8):
                    kb.dma("sp", a[:], attnT[c * 128:(c + 1) * 128, :], reads=[], writes=[a])
                    V("dve", lambda e: e.tensor_copy(b[:], a[:]), [a], [b])
                    kb.dma("sp", dbg_attn[c * 128:(c + 1) * 128, :], b[:], reads=[b], writes=[])
                kb.fence()
        if on("p3"): run_phase(phase3, l, xsrc, xdst, need_ctx)
    kb.fence(engines=["sp"])
    top.close()
    return nc, kb


_CACHE = {}


def kernel(**inputs):
    inputs = {k: np.asarray(v) for k, v in inputs.items()}
    maps = _prep(inputs)
    if "nc" not in _CACHE:
        _CACHE["nc"] = build()[0]
    nc = _CACHE["nc"]
    res = run_bass_kernel_spmd(nc, maps, core_ids=list(range(NB)))
    out = np.stack([np.ascontiguousarray(res.results[b]["outT"].T) for b in range(NB)], axis=0)
    return out.astype(np.float32)
```

```python
import math
from contextlib import ExitStack
import numpy as np
import concourse.bass as bass
import concourse.mybir as mybir
from concourse.bass_utils import run_bass_kernel_spmd

F32 = mybir.dt.float32
BF16 = mybir.dt.bfloat16
ALU = mybir.AluOpType
AF = mybir.ActivationFunctionType
AX = mybir.AxisListType

D = 1024
T = 4096
NCTX = 256
NK = T + NCTX
DEPTH = 2
NB = 4
EPS = 1e-6
NEG = -30000.0
NPP = 216
NCST = 1730
TM0 = 2176
NCOL = 3328


class Buf:
    __slots__ = ("name", "t", "lw", "rd", "dsem", "dcnt", "space")

    def __init__(self, name, t, space):
        self.name = name
        self.t = t
        self.space = space
        self.lw = None
        self.rd = []
        self.dsem = None
        self.dcnt = 0

    def __getitem__(self, key):
        return self.t[key]


class KB:
    SEM_LIMIT = 32000

    def __init__(self, nc):
        self.nc = nc
        self.engs = {"pe": nc.tensor, "act": nc.scalar, "dve": nc.vector,
                     "pool": nc.gpsimd, "sp": nc.sync}
        self.esem = {}
        self.ecnt = {}
        self.nsem = 0
        self.retired = []
        self.free_dsems = []
        self.pe_sems = set()
        for e in self.engs:
            self._new_esem(e)
        self.seen = {e: {} for e in self.engs}
        self.dsems = []
        self.nwaits = 0
        self.ninst = 0

    def _alloc_sem(self, name):
        self.nsem += 1
        return self.nc.alloc_semaphore(f"{name}_{self.nsem}")

    def _new_esem(self, e):
        if e in self.esem and self.ecnt[e] > 0:
            self.retired.append((self.esem[e], self.ecnt[e]))
        self.esem[e] = self._alloc_sem("e" + e)
        self.ecnt[e] = 0
        if e == "pe":
            self.pe_sems.add(self.esem[e].num)

    def sbuf(self, st, name, shape, dtype):
        self.nbuf = getattr(self, "nbuf", 0) + 1
        t = st.enter_context(self.nc.sbuf_tensor(f"s{self.nbuf}_{name}", list(shape), dtype))
        b = Buf(name, t, "sbuf")
        st.callback(self.release, b)
        return b

    def psum_all(self):
        t = self.nc.alloc_psum_tensor("psum_all", [128, 4096], F32)
        return t, [Buf(f"bank{i}", t, "psum") for i in range(8)]

    def dram(self, name, shape, dtype, kind="Internal"):
        t = self.nc.dram_tensor(name, list(shape), dtype, kind=kind)
        return Buf(name, t, "dram")

    def release(self, b):
        if b.dsem is not None:
            self.free_dsems.append((b.dsem, b.dcnt))
            if b in self.dsems:
                self.dsems.remove(b)
            self.retired.append((b.dsem, b.dcnt))
            b.dsem = None

    def _wait(self, e, deps):
        eng = self.engs[e]
        seen = self.seen[e]
        best = {}
        for d in deps:
            if d is None:
                continue
            s, v = d
            if e == "pe" and s.num in self.pe_sems:
                continue
            if best.get(s.num, (None, 0))[1] < v:
                best[s.num] = (s, v)
        for s, v in best.values():
            if seen.get(s.num, 0) < v:
                eng.wait_ge(s, v)
                seen[s.num] = v
                self.nwaits += 1

    def _deps(self, reads, writes, e=None):
        deps = []
        for b in reads:
            deps.append(b.lw)
            if b.space == "psum" and e is not None:
                mine = self.esem[e].num
                deps.extend(t for t in b.rd if t[0].num != mine)
        for b in writes:
            deps.append(b.lw)
            deps.extend(b.rd)
        return deps

    def _commit(self, tok, reads, writes):
        for b in reads:
            b.rd.append(tok)
            if len(b.rd) > 48:
                m = {}
                for s, v in b.rd:
                    if m.get(s.num, (None, 0))[1] < v:
                        m[s.num] = (s, v)
                b.rd = list(m.values())
        for b in writes:
            b.lw = tok
            b.rd = []

    def op(self, e, fn, reads=(), writes=()):
        if self.ecnt[e] >= self.SEM_LIMIT:
            self._new_esem(e)
        self._wait(e, self._deps(reads, writes, e))
        ins = fn(self.engs[e])
        self.ecnt[e] += 1
        ins.then_inc(self.esem[e], 1)
        tok = (self.esem[e], self.ecnt[e])
        self._commit(tok, reads, writes)
        self.ninst += 1
        return tok

    def dma(self, q, out_ap, in_ap, reads=(), writes=()):
        owner = None
        for b in list(writes) + list(reads):
            if b.space == "sbuf":
                owner = b
                break
        if owner is None:
            owner = (list(writes) + list(reads))[0]
        if owner.dsem is None or owner.dcnt >= self.SEM_LIMIT:
            if owner.dsem is not None:
                self.retired.append((owner.dsem, owner.dcnt))
                owner.dsem = None
            while self.free_dsems:
                s, c = self.free_dsems.pop()
                if c < self.SEM_LIMIT - 2000:
                    owner.dsem, owner.dcnt = s, c
                    break
            if owner.dsem is None:
                owner.dsem = self._alloc_sem("d")
                owner.dcnt = 0
            if owner not in self.dsems:
                self.dsems.append(owner)
        self._wait(q, self._deps(reads, writes))
        ins = self.engs[q].dma_start(out=out_ap, in_=in_ap)
        owner.dcnt += 16
        ins.then_inc(owner.dsem, 16)
        tok = (owner.dsem, owner.dcnt)
        self._commit(tok, reads, writes)
        self.ninst += 1
        return tok

    def fence(self, engines=None):
        m = {}
        for s, v in self.retired:
            if m.get(s.num, (None, 0))[1] < v:
                m[s.num] = (s, v)
        for e in self.engs:
            if self.ecnt[e] > 0:
                m[self.esem[e].num] = (self.esem[e], self.ecnt[e])
        for b in self.dsems:
            if b.dcnt > 0 and m.get(b.dsem.num, (None, 0))[1] < b.dcnt:
                m[b.dsem.num] = (b.dsem, b.dcnt)
        targets = list(m.values())
        for e in (engines or self.engs):
            self._wait(e, targets)
        if engines is None:
            self.retired = []


def _partner(dh):
    half, q = dh // 2, dh // 4
    p = np.arange(dh)
    loc = p % half
    return np.where(loc < q, p + q, p - q), np.where(loc < q, -1.0, 1.0).astype(np.float32)


def _rope_tables(dh, nrep):
    half, q = dh // 2, dh // 4
    pos = np.arange(T)
    rows = (pos // 64).astype(np.float32)
    cols = (pos % 64).astype(np.float32)
    freqs = (10000.0 ** (-np.arange(q, dtype=np.float32) / q)).astype(np.float32)
    _, sgn = _partner(dh)
    cos = np.zeros((dh, T), np.float32)
    sin = np.zeros((dh, T), np.float32)
    for d in range(dh):
        pv = rows if d < half else cols
        ang = (pv * freqs[d % q]).astype(np.float32)
        cos[d] = np.cos(ang)
        sin[d] = np.sin(ang) * sgn[d]
    return np.stack([np.tile(cos, (nrep, 1)), np.tile(sin, (nrep, 1))]).astype(np.float32)


def _fm_groups():
    p64, _ = _partner(64)
    p32, _ = _partner(32)
    g = []
    ar = np.arange
    for nm, c0 in (("Aq0", 0), ("Aq1", 128), ("Ak", 256)):
        g.append((nm, c0 + ar(128)))
    for h in range(4):
        g.append((f"Bq{h}", 512 + h * 64 + ar(64)))
    for h in range(4):
        g.append((f"Bk{h}", 768 + h * 64 + ar(64)))
    for nm, c0 in (("Cq0", 1280), ("Cq1", 1408), ("Ck0", 1536), ("Ck1", 1664), ("Cg0", 2048), ("Cg1", 2176),
                   ("Dq0", 2304), ("Dq1", 2432), ("Dk0", 2560), ("Dk1", 2688)):
        g.append((nm, c0 + ar(128)))
    return g


FMG = _fm_groups()
FMOFF = {}
_o = 0
for _n, _c in FMG:
    FMOFF[_n] = (_o, len(_c))
    _o += len(_c)
assert _o == TM0
TMCOLS = np.concatenate([1024 + np.arange(256), 2816 + np.arange(256), 1792 + np.arange(256),
                         1536 + np.arange(256), 384 + np.arange(128)])


def _dtables(rpb):
    kc = np.arange(64)[:, None]
    c = np.arange(64)[None, :]
    cstart = np.clip(c - 8, 0, 48)
    colv = (kc >= cstart) & (kc < cstart + 16)
    dc = np.clip(kc - c + 15, 0, 30)
    bias = np.zeros((DEPTH, 4, 128, 12, 128), np.float32)
    mask = np.full((128, 12, 128), NEG, np.float32)
    for v in range(12):
        delta = v - 2 if v < 5 else v - 8
        for krp in range(2):
            for rp in range(2):
                dr = 2 * delta + krp - rp
                ok = (-4 <= dr <= 3) if v < 5 else (abs(dr) <= 7)
                if abs(dr) > 7:
                    continue
                bias[:, :, krp * 64:(krp + 1) * 64, v, rp * 64:(rp + 1) * 64] = rpb[:, :, dr + 7][:, :, dc]
                if ok:
                    mask[krp * 64:(krp + 1) * 64, v, rp * 64:(rp + 1) * 64] = np.where(colv, 0.0, NEG)
    return bias, mask


def _consts():
    c = np.zeros((128, NCST), np.float32)
    i = np.arange(128)
    c[:, 0:128] = np.eye(128)
    c[:, 128:256] = (i[:, None] // 64 == i[None, :] // 64)
    c[:, 256:384] = (i[:, None] // 32 == i[None, :] // 32)
    c[:, 384:512] = 1.0
    j, q = i[:, None], i[None, :]
    c[:, 512:640] = np.where(j >= q, 0.0, NEG)
    c[:, 640:768] = np.where(j <= q, 0.0, NEG)
    c[:, 768:896] = q - j
    c[:, 896:1024] = (q >= j)
    c[:, 1024:1152] = j - q
    c[:, 1152:1280] = (j >= q)
    c[:, 1280:1408] = q + 1
    c[:, 1408:1536] = 128 - q
    c[:, 1536] = 127 - i
    c[:, 1537] = i
    p64, _ = _partner(64)
    p32, _ = _partner(32)
    for m in range(128):
        c[(m // 64) * 64 + p64[m % 64], 1538 + m] = 1.0
    for m in range(64):
        c[(m // 32) * 32 + p32[m % 32], 1666 + m] = 1.0
    return c


def _prep(inp):
    f = lambda a: np.ascontiguousarray(a, dtype=np.float32)
    x, c, ctx, c_ctx = inp["x"], inp["c"], inp["ctx"], inp["c_ctx"]
    shared = {}
    shared["w_mod_r"] = f(inp["w_mod"].reshape(DEPTH, 8, 128, 12, 512).transpose(0, 3, 2, 1, 4))
    fmcols = np.concatenate([cc for _, cc in FMG] + [TMCOLS])
    shared["w_in_r"] = f(inp["w_in"][:, :, fmcols].reshape(DEPTH, 8, 128, 32, NCOL // 32).transpose(0, 3, 2, 1, 4))
    shared["w_out_r"] = f(inp["w_out"].reshape(DEPTH, 8, 128, D).transpose(0, 2, 1, 3))
    shared["w1_r"] = f(inp["w_mlp1"].reshape(DEPTH, 8, 128, 4 * D).transpose(0, 2, 1, 3))
    shared["w2_r"] = f(inp["w_mlp2"].reshape(DEPTH, 32, 128, D).transpose(0, 2, 1, 3))
    pp = np.zeros((DEPTH, 128, NPP), np.float32)
    p = np.arange(128)
    p64, _ = _partner(64)
    p32, _ = _partner(32)
    for l in range(DEPTH):
        pp[l, :, 0:8] = inp["norm1_g"][l].reshape(8, 128).T
        pp[l, :, 8:16] = inp["norm2_g"][l].reshape(8, 128).T
        pp[l, :, 16:64] = inp["b_mod"][l].reshape(48, 128).T
        pp[l, :, 64] = inp["a_qnorm_g"][l][p % 64]
        pp[l, :, 65] = inp["a_qnorm_g"][l][p64[p % 64]]
        pp[l, :, 66] = inp["a_knorm_g"][l][p % 64]
        pp[l, :, 67] = inp["a_knorm_g"][l][p64[p % 64]]
        pp[l, :, 68] = inp["b_qnorm_g"][l][p % 32]
        pp[l, :, 69] = inp["b_qnorm_g"][l][p32[p % 32]]
        pp[l, :, 70] = inp["b_knorm_g"][l][p % 32]
        pp[l, :, 71] = inp["b_knorm_g"][l][p32[p % 32]]
        pp[l, :, 72] = inp["d_qnorm_g"][l][p % 64]
        pp[l, :, 73] = inp["d_knorm_g"][l][p % 64]
        pp[l, :, 74] = inp["c_gn_g"][l][p % 64]
        pp[l, :, 75] = inp["b_subln_g"][l][p % 64]
        pp[l, :, 76:80] = inp["a_sink"][l][None, :]
        pp[l, :, 80:84] = inp["c_decay_fwd"][l][None, :]
        pp[l, :, 84:88] = inp["c_decay_bwd"][l][None, :]
        pp[l, :, 88:120] = inp["b_lambda_q1"][l][None, :]
        pp[l, :, 120:152] = inp["b_lambda_k1"][l][None, :]
        pp[l, :, 152:184] = inp["b_lambda_q2"][l][None, :]
        pp[l, :, 184:216] = inp["b_lambda_k2"][l][None, :]
    shared["pp"] = pp
    dbias, dmask = _dtables(np.asarray(inp["d_rpb"], np.float32))
    shared["dbias"] = dbias
    shared["dmask"] = dmask
    shared["cst"] = _consts()
    shared["ropeA"] = _rope_tables(64, 2)
    shared["ropeB"] = _rope_tables(32, 2)
    maps = []
    for b in range(NB):
        m = dict(shared)
        m["xin"] = f(np.concatenate([x[b].T, ctx[b].T], axis=1))
        cv = np.stack([c[b].reshape(8, 128).T, c_ctx.reshape(8, 128).T], axis=-1)
        m["cvec"] = f(cv)
        maps.append(m)
    return maps


def build(nlayers=DEPTH, debug=False, only=None):
    nc = bass.Bass("TRN2", target_bir_lowering=False)
    kb = KB(nc)
    EI = "ExternalInput"
    xin = kb.dram("xin", [D, NK], F32, EI)
    cvec_d = kb.dram("cvec", [128, 8, 2], F32, EI)
    w_mod_d = kb.dram("w_mod_r", [DEPTH, 12, 128, 8, 512], F32, EI)
    w_in_d = kb.dram("w_in_r", [DEPTH, 32, 128, 8, NCOL // 32], F32, EI)
    w_out_d = kb.dram("w_out_r", [DEPTH, 128, 8, D], F32, EI)
    w1_d = kb.dram("w1_r", [DEPTH, 128, 8, 4 * D], F32, EI)
    w2_d = kb.dram("w2_r", [DEPTH, 128, 32, D], F32, EI)
    pp_d = kb.dram("pp", [DEPTH, 128, NPP], F32, EI)
    dbias_d = kb.dram("dbias", [DEPTH, 4, 128, 12, 128], F32, EI)
    dmask_d = kb.dram("dmask", [128, 12, 128], F32, EI)
    cst_d = kb.dram("cst", [128, NCST], F32, EI)
    ropeA_d = kb.dram("ropeA", [2, 128, T], F32, EI)
    ropeB_d = kb.dram("ropeB", [2, 64, T], F32, EI)
    outT = kb.dram("outT", [D, T], F32, "ExternalOutput")
    xmid = kb.dram("xmid", [D, NK], F32, "ExternalOutput" if debug else "Internal")
    attnT = kb.dram("attnT", [D, NK], BF16)
    dbg_attn = kb.dram("dbg_attn", [D, NK], F32, "ExternalOutput") if debug else None
    wo_bf = kb.dram("wo_bf", [128, 8, D], BF16)
    w1_bf = kb.dram("w1_bf", [32, 128, 8, 128], BF16)
    w2_bf = kb.dram("w2_bf", [8, 128, 32, 128], BF16)
    fm = {}
    for n, cc in FMG:
        if n.endswith("r"):
            continue
        fm[n] = kb.dram("fm_" + n, [len(cc), NK], BF16)
    VA_d = kb.dram("VA_d", [NK, 2, 128], BF16)
    VB_d = kb.dram("VB_d", [NK, 4, 128], BF16)
    VD_d = kb.dram("VD_d", [NK, 4, 128], BF16)
    VC_d = kb.dram("VC_d", [NK, 256], BF16)
    KF_d = kb.dram("KF_d", [NK, 256], BF16)
    KBk_d = kb.dram("KBk_d", [NK, 256], BF16)

    pst, PB = kb.psum_all()

    def bank(i, p0=0, p1=128, w=512):
        return pst[p0:p1, i * 512:i * 512 + w]

    top = ExitStack()
    cst = kb.sbuf(top, "cst", [128, NCST], F32)
    cb = kb.sbuf(top, "cstbf", [128, 768], BF16)
    pp = kb.sbuf(top, "pp", [128, NPP], F32)
    der = kb.sbuf(top, "der", [128, 64], F32)
    modT = kb.sbuf(top, "modT", [128, 48, 2], F32)
    gm = kb.sbuf(top, "gm", [128, 2, 8, 2], F32)
    DT = kb.sbuf(top, "DT", [128, 8, 128], F32)
    DQ = kb.sbuf(top, "DQ", [64, 8, 128], F32)
    G128 = kb.sbuf(top, "G128", [64, 8, 64], F32)
    kdec = kb.sbuf(top, "kdec", [128, 8], F32)
    kb.dma("sp", cst[:], cst_d[:], reads=[cst_d], writes=[cst])
    kb.op("dve", lambda e: e.tensor_copy(cb[:], cst[:, 0:768]), reads=[cst], writes=[cb])
    permbf = kb.sbuf(top, "permbf", [128, 192], BF16)
    kb.op("dve", lambda e: e.tensor_copy(permbf[:], cst[:, 1538:1730]), reads=[cst], writes=[permbf])
    ident = cb[:, 0:128]
    blk64 = cb[:, 128:256]
    blk32 = cb[0:64, 256:320]
    ones_bf = cb[:, 384:512]
    amask = {"prev": cb[:, 512:640], "next": cb[:, 640:768]}

    C_AQ, C_AQP, C_AK, C_AKP, C_BQ, C_BQP, C_BK, C_BKP, C_DQ, C_DK, C_CGN, C_BSUB = range(12)
    C_ESINK = 12
    C_LGF = 16
    C_NLAM = 24
    C_TMP = 32

    def V(e, fn, reads, writes):
        return kb.op(e, fn, reads, writes)

    def phase0(l, st):
        lam_init = 0.8 - 0.6 * math.exp(-0.3 * l)
        kb.dma("sp", pp[:], pp_d[l], reads=[pp_d], writes=[pp])
        cv = kb.sbuf(st, "cv", [128, 8, 2], F32)
        sc = kb.sbuf(st, "sc", [128, 8, 2], F32)
        kb.dma("sp", cv[:], cvec_d[:], reads=[cvec_d], writes=[cv])
        V("act", lambda e: e.activation(sc[:], cv[:], AF.Silu), [cv], [sc])
        wm = [kb.sbuf(st, f"wm{i}", [128, 8, 512], F32) for i in range(2)]
        modrow = kb.sbuf(st, "modrow", [2, 6 * D], F32)
        pm = pst[:, 0:96].rearrange("p (j s) -> p j s", s=2)
        for ch in range(12):
            w = wm[ch % 2]
            kb.dma("sp", w[:], w_mod_d[l, ch], reads=[w_mod_d], writes=[w])
            rb = 1 + ch % 2

            def mm(e, w=w, rb=rb):
                ins = None
                for kc in range(8):
                    ins = e.matmul(bank(rb, 0, 2, 512), sc[:, kc, :], w[:, kc, :], start=(kc == 0), stop=(kc == 7))
                return ins
            V("pe", mm, [w, sc], [PB[rb]])
            V("act", lambda e, ch=ch, rb=rb: e.copy(modrow[:, ch * 512:(ch + 1) * 512], bank(rb, 0, 2, 512)), [PB[rb]], [modrow])

        def flip(e):
            ins = None
            for j in range(48):
                ins = e.matmul(pm[:, j, :], modrow[:, j * 128:(j + 1) * 128], cst[0:2, 0:2], start=True, stop=True)
            return ins
        V("pe", flip, [modrow, cst], [PB[0]])
        V("dve", lambda e: e.tensor_tensor(modT[:], pm, pp[:, 16:64].unsqueeze(2).to_broadcast([128, 48, 2]), ALU.add),
          [PB[0], pp], [modT])
        for k, (gcol, mj) in enumerate(((0, 8), (8, 32))):
            V("dve", lambda e, k=k, mj=mj: e.tensor_scalar(gm[:, k], modT[:, mj:mj + 8, :], 1.0, None, ALU.add), [modT], [gm])
            V("dve", lambda e, k=k, gcol=gcol: e.tensor_tensor(
                gm[:, k], gm[:, k], pp[:, gcol:gcol + 8].unsqueeze(2).to_broadcast([128, 8, 2]), ALU.mult), [gm, pp], [gm])
        s64, s32 = 64 ** -0.5, 32 ** -0.5
        for dst, src, mul in ((C_AQ, 64, s64), (C_AQP, 65, s64), (C_AK, 66, 1.0), (C_AKP, 67, 1.0),
                              (C_BQ, 68, s32), (C_BQP, 69, s32), (C_BK, 70, 1.0), (C_BKP, 71, 1.0),
                              (C_DQ, 72, s64), (C_DK, 73, 1.0), (C_CGN, 74, 1.0), (C_BSUB, 75, 1.0 - lam_init)):
            V("dve", lambda e, dst=dst, src=src, mul=mul: e.tensor_scalar(
                der[:, dst:dst + 1], pp[:, src:src + 1], float(mul), None, ALU.mult), [pp], [der])
        V("act", lambda e: e.activation(der[:, C_ESINK:C_ESINK + 4], pp[:, 76:80], AF.Exp), [pp], [der])
        for k, (a0, b0) in enumerate(((88, 120), (152, 184))):
            V("dve", lambda e, a0=a0, b0=b0: e.tensor_tensor(der[:, C_TMP:C_TMP + 32], pp[:, a0:a0 + 32], pp[:, b0:b0 + 32], ALU.mult),
              [pp], [der])
            V("dve", lambda e, k=k: e.tensor_reduce(der[:, C_NLAM + 1 + k:C_NLAM + 2 + k], der[:, C_TMP:C_TMP + 32], AX.X, ALU.add),
              [der], [der])
        V("act", lambda e: e.activation(der[:, C_NLAM + 1:C_NLAM + 3], der[:, C_NLAM + 1:C_NLAM + 3], AF.Exp), [der], [der])
        V("dve", lambda e: e.tensor_tensor(der[:, C_NLAM:C_NLAM + 1], der[:, C_NLAM + 2:C_NLAM + 3], der[:, C_NLAM + 1:C_NLAM + 2], ALU.subtract),
          [der], [der])
        V("dve", lambda e: e.tensor_scalar(der[:, C_NLAM:C_NLAM + 1], der[:, C_NLAM:C_NLAM + 1], float(-lam_init), None, ALU.add),
          [der], [der])
        V("act", lambda e: e.activation(der[:, C_LGF:C_LGF + 8], pp[:, 80:88], AF.Exp, scale=-1.0), [pp], [der])
        V("act", lambda e: e.activation(der[:, C_LGF:C_LGF + 8], der[:, C_LGF:C_LGF + 8], AF.Ln, bias=1.0), [der], [der])
        V("dve", lambda e: e.tensor_scalar(der[:, C_LGF:C_LGF + 8], der[:, C_LGF:C_LGF + 8], -1.0, None, ALU.mult), [der], [der])
        for dr_ in range(2):
            rel = cst[:, 768:896] if dr_ == 0 else cst[:, 1024:1152]
            tri = cst[:, 896:1024] if dr_ == 0 else cst[:, 1152:1280]
            iq = cst[0:64, 1280:1408] if dr_ == 0 else cst[0:64, 1408:1536]
            for h in range(4):
                k = dr_ * 4 + h
                lgc = der[:, C_LGF + k:C_LGF + k + 1]
                V("act", lambda e, k=k, rel=rel, lgc=lgc: e.activation(DT[:, k, :], rel, AF.Exp, scale=lgc), [der, cst], [DT])
                V("dve", lambda e, k=k, tri=tri: e.tensor_tensor(DT[:, k, :], DT[:, k, :], tri, ALU.mult), [DT, cst], [DT])
                V("act", lambda e, k=k, iq=iq: e.activation(DQ[:, k, :], iq, AF.Exp, scale=der[0:64, C_LGF + k:C_LGF + k + 1]), [der, cst], [DQ])
                V("act", lambda e, k=k, dr_=dr_, lgc=lgc: e.activation(kdec[:, k:k + 1], cst[:, 1536 + dr_:1537 + dr_], AF.Exp, scale=lgc),
                  [der, cst], [kdec])
        V("dve", lambda e: e.tensor_scalar(kdec[:], kdec[:], 0.125, None, ALU.mult), [kdec], [kdec])
        V("act", lambda e: e.activation(der[:, C_TMP:C_TMP + 8], der[:, C_LGF:C_LGF + 8], AF.Exp, scale=128.0), [der], [der])
        V("dve", lambda e: e.tensor_copy(G128[:], der[0:64, C_TMP:C_TMP + 8].unsqueeze(2).to_broadcast([64, 8, 64])), [der], [G128])

    def phase1(l, st, xsrc):
        win = kb.sbuf(st, "win", [128, 8, NCOL], BF16)
        wst = [kb.sbuf(st, f"wst{i}", [128, 8, 104], F32) for i in range(2)]
        for i in range(32):
            w = wst[i % 2]
            kb.dma("sp", w[:], w_in_d[l, i], reads=[w_in_d], writes=[w])
            if i % 2 == 0:
                V("act", lambda e, w=w, i=i: e.copy(win[:, :, i * 104:(i + 1) * 104], w[:]), [w], [win])
            else:
                V("pool", lambda e, w=w, i=i: e.tensor_copy(win[:, :, i * 104:(i + 1) * 104], w[:]), [w], [win])
        xT = [kb.sbuf(st, f"xT{i}", [128, 8, 512], F32) for i in range(2)]
        sq = kb.sbuf(st, "sq", [128, 8, 512], BF16)
        hT = kb.sbuf(st, "hT", [128, 8, 512], BF16)
        rs = kb.sbuf(st, "rs", [128, 512], F32)
        tmp = [kb.sbuf(st, f"tmp{i}", [128, 512], F32) for i in range(2)]
        rsg = [kb.sbuf(st, f"rsg{i}", [128, 512], F32) for i in range(2)]
        ta = [kb.sbuf(st, f"ta{i}", [128, 512], F32) for i in range(2)]
        tb = [kb.sbuf(st, f"tb{i}", [128, 512], F32) for i in range(2)]
        sqg = [kb.sbuf(st, f"sqg{i}", [128, 512], BF16) for i in range(2)]
        qb16 = [kb.sbuf(st, f"qb16{i}", [128, 512], BF16) for i in range(2)]
        og = [kb.sbuf(st, f"og{i}", [128, 512], BF16) for i in range(4)]
        rA = [kb.sbuf(st, f"rA{i}", [128, 2, 512], F32) for i in range(2)]
        rB = [kb.sbuf(st, f"rB{i}", [64, 2, 512], F32) for i in range(2)]
        VAs = [kb.sbuf(st, f"VAs{i}", [128, 2, 128], BF16) for i in range(2)]
        VBs = [kb.sbuf(st, f"VBs{i}", [128, 4, 128], BF16) for i in range(2)]
        VDs = [kb.sbuf(st, f"VDs{i}", [128, 4, 128], BF16) for i in range(2)]
        VCs = [kb.sbuf(st, f"VCs{i}", [128, 256], BF16) for i in range(2)]
        KFs = [kb.sbuf(st, f"KFs{i}", [128, 256], BF16) for i in range(2)]
        KBs = [kb.sbuf(st, f"KBs{i}", [128, 256], BF16) for i in range(2)]
        for tl in VAs + VBs + VDs:
            V("pool", lambda e, tl=tl: e.memset(tl[:], 1.0), [], [tl])
        cnt = {"b": 0, "o": 0, "g": 0}

        def nb():
            cnt["b"] = (cnt["b"] + 1) % 8
            return cnt["b"]

        def nog():
            cnt["o"] = (cnt["o"] + 1) % 4
            return og[cnt["o"]]

        def proj(bi, col0, P, Wb):
            def f(e):
                ins = None
                for kc in range(8):
                    ins = e.matmul(bank(bi, 0, P, Wb), win[:, kc, col0:col0 + P], hT[:, kc, :Wb], start=(kc == 0), stop=(kc == 7))
                return ins
            V("pe", f, [win, hT], [PB[bi]])

        def store(name, o, P, c0, Wb):
            kb.dma("pool", fm[name][:, c0:c0 + Wb], o[0:P, :Wb], reads=[o], writes=[])

        pend = []

        def flush(keep=0):
            while len(pend) > keep:
                pend.pop(0)()

        def normgrp(name, P, dh, gcol, c0, Wb, rope=None, gpcol=None):
            k = cnt["g"] = (cnt["g"] + 1) % 2
            col0 = FMOFF[name][0]
            bo = nb()
            proj(bo, col0, P, Wb)
            po = bank(bo, 0, P, Wb)
            V("act", lambda e: e.activation(sqg[k][0:P, :Wb], po, AF.Square), [PB[bo]], [sqg[k]])
            br = None
            if rope is not None:
                V("act", lambda e: e.copy(qb16[k][0:P, :Wb], po), [PB[bo]], [qb16[k]])
            rbuf = rope_buf[0]

            def stage2():
                nonlocal br
                if rope is not None:
                    br = nb()
                    perm = permbf[:, 0:128] if P == 128 else permbf[0:64, 128:192]
                    V("pe", lambda e: e.matmul(bank(br, 0, P, Wb), perm, qb16[k][0:P, :Wb], start=True, stop=True), [qb16[k], permbf], [PB[br]])
                bs = nb()
                blk = blk64[0:P, 0:P] if dh == 64 else blk32
                V("pe", lambda e: e.matmul(bank(bs, 0, P, Wb), blk, sqg[k][0:P, :Wb], start=True, stop=True), [sqg[k], cb], [PB[bs]])
                V("act", lambda e: e.activation(rsg[k][0:P, :Wb], bank(bs, 0, P, Wb), AF.Ln, bias=EPS, scale=1.0 / dh), [PB[bs]], [rsg[k]])
                V("act", lambda e: e.activation(rsg[k][0:P, :Wb], rsg[k][0:P, :Wb], AF.Exp, scale=-0.5), [rsg[k]], [rsg[k]])
                o = nog()
                if rope is None:
                    V("dve", lambda e: e.scalar_tensor_tensor(o[0:P, :Wb], po, der[0:P, gcol:gcol + 1], rsg[k][0:P, :Wb], ALU.mult, ALU.mult),
                      [PB[bo], der, rsg[k]], [o])
                else:
                    pr = bank(br, 0, P, Wb)
                    V("dve", lambda e: e.scalar_tensor_tensor(ta[k][0:P, :Wb], po, der[0:P, gcol:gcol + 1], rsg[k][0:P, :Wb], ALU.mult, ALU.mult),
                      [PB[bo], der, rsg[k]], [ta[k]])
                    V("dve", lambda e: e.scalar_tensor_tensor(tb[k][0:P, :Wb], pr, der[0:P, gpcol:gpcol + 1], rsg[k][0:P, :Wb], ALU.mult, ALU.mult),
                      [PB[br], der, rsg[k]], [tb[k]])
                    V("dve", lambda e: e.tensor_tensor(ta[k][0:P, :Wb], ta[k][0:P, :Wb], rope[0:P, 0, :Wb], ALU.mult), [ta[k], rbuf], [ta[k]])
                    V("dve", lambda e: e.tensor_tensor(tb[k][0:P, :Wb], tb[k][0:P, :Wb], rope[0:P, 1, :Wb], ALU.mult), [tb[k], rbuf], [tb[k]])
                    V("dve", lambda e: e.tensor_tensor(o[0:P, :Wb], ta[k][0:P, :Wb], tb[k][0:P, :Wb], ALU.add), [ta[k], tb[k]], [o])
                store(name, o, P, c0, Wb)
            pend.append(stage2)
            flush(keep=1)

        def plaingrp(name, func, scale, c0, Wb):
            bo = nb()
            proj(bo, FMOFF[name][0], 128, Wb)
            o = nog()
            V("act", lambda e: e.activation(o[:, :Wb], bank(bo, 0, 128, Wb), func, scale=scale), [PB[bo]], [o])
            store(name, o, 128, c0, Wb)

        rope_buf = [None]
        import os
        CUT = int(os.environ.get("P1CUT", "99"))
        for blk_i in range(9 if CUT >= 99 else 1):
            c0 = blk_i * 512
            Wb = 512 if blk_i < 8 else 256
            s = 0 if blk_i < 8 else 1
            x_t = xT[blk_i % 2]
            kb.dma("sp", x_t[:, :, :Wb], xsrc[:, c0:c0 + Wb].rearrange("(c p) t -> p c t", p=128), reads=[xsrc], writes=[x_t])
            ra, rb = rA[blk_i % 2], rB[blk_i % 2]
            if blk_i < 8:
                kb.dma("sp", ra[:], ropeA_d[:, :, c0:c0 + 512].rearrange("a p t -> p a t"), reads=[ropeA_d], writes=[ra])
                kb.dma("sp", rb[:], ropeB_d[:, :, c0:c0 + 512].rearrange("a p t -> p a t"), reads=[ropeB_d], writes=[rb])
            if CUT < 2:
                continue
            for c in range(8):
                V("act", lambda e, c=c: e.activation(sq[:, c, :Wb], x_t[:, c, :Wb], AF.Square), [x_t], [sq])
            b0 = nb()

            def ssm(e):
                ins = None
                for c in range(8):
                    ins = e.matmul(bank(b0, 0, 128, Wb), ones_bf, sq[:, c, :Wb], start=(c == 0), stop=(c == 7))
                return ins
            V("pe", ssm, [sq, cb], [PB[b0]])
            V("act", lambda e: e.activation(rs[:, :Wb], bank(b0, 0, 128, Wb), AF.Ln, bias=EPS, scale=1.0 / D), [PB[b0]], [rs])
            V("act", lambda e: e.activation(rs[:, :Wb], rs[:, :Wb], AF.Exp, scale=-0.5), [rs], [rs])
            for c in range(8):
                t_ = tmp[c % 2]
                V("dve", lambda e, c=c, t_=t_: e.scalar_tensor_tensor(t_[:, :Wb], x_t[:, c, :Wb], gm[:, 0, c, s:s + 1], rs[:, :Wb], ALU.mult, ALU.mult),
                  [x_t, gm, rs], [t_])
                V("act", lambda e, c=c, t_=t_: e.activation(hT[:, c, :Wb], t_[:, :Wb], AF.Identity, bias=modT[:, c, s:s + 1], scale=1.0),
                  [t_, modT], [hT])
            isl = blk_i < 8
            rope_buf[0] = ra
            if CUT < 3:
                continue
            skipq = (blk_i == 8 and l == DEPTH - 1)
            if not skipq:
                normgrp("Aq0", 128, 64, C_AQ, c0, Wb, ra if isl else None, C_AQP)
                normgrp("Aq1", 128, 64, C_AQ, c0, Wb, ra if isl else None, C_AQP)
            normgrp("Ak", 128, 64, C_AK, c0, Wb, ra if isl else None, C_AKP)
            rope_buf[0] = rb
            for h in range(4):
                if not skipq:
                    normgrp(f"Bq{h}", 64, 32, C_BQ, c0, Wb, rb if isl else None, C_BQP)
            for h in range(4):
                normgrp(f"Bk{h}", 64, 32, C_BK, c0, Wb, rb if isl else None, C_BKP)
            if CUT < 5:
                continue
            flush()
            if not skipq:
                plaingrp("Cq0", AF.Identity, 1.0, c0, Wb)
            if not skipq:
                plaingrp("Cq1", AF.Identity, 1.0, c0, Wb)
            plaingrp("Ck0", AF.Identity, 0.125, c0, Wb)
            plaingrp("Ck1", AF.Identity, 0.125, c0, Wb)
            if not skipq:
                plaingrp("Cg0", AF.Silu, 1.0, c0, Wb)
            if not skipq:
                plaingrp("Cg1", AF.Silu, 1.0, c0, Wb)
            if CUT < 6:
                continue
            if not skipq:
                normgrp("Dq0", 128, 64, C_DQ, c0, Wb)
            if not skipq:
                normgrp("Dq1", 128, 64, C_DQ, c0, Wb)
            normgrp("Dk0", 128, 64, C_DK, c0, Wb)
            normgrp("Dk1", 128, 64, C_DK, c0, Wb)
            flush()
            if CUT < 7:
                continue
            for tt in range(Wb // 128):
                tok0 = c0 + tt * 128
                k = tt % 2
                b1, b2, b3 = nb(), nb(), nb()

                def tmm(e, tt=tt, b1=b1, b2=b2, b3=b3):
                    ins = None
                    for bi, cc, w in ((b1, TM0, 512), (b2, TM0 + 512, 512), (b3, TM0 + 1024, 128)):
                        for kc in range(8):
                            ins = e.matmul(bank(bi, 0, 128, w), hT[:, kc, tt * 128:(tt + 1) * 128], win[:, kc, cc:cc + w],
                                           start=(kc == 0), stop=(kc == 7))
                    return ins
                V("pe", tmm, [hT, win], [PB[b1], PB[b2], PB[b3]])
                p1 = bank(b1)
                p2 = bank(b2)
                p3 = bank(b3, 0, 128, 128)
                V("act", lambda e, k=k, p2=p2: e.copy(VCs[k][:], p2[:, 0:256]), [PB[b2]], [VCs[k]])
                if CUT < 8:
                    kb.dma("pool", VC_d[tok0:tok0 + 128], VCs[k][:], reads=[VCs[k]], writes=[])
                    continue
                V("act", lambda e, k=k, p1=p1: e.copy(VBs[k][:, :, 0:64], p1[:, 0:256].rearrange("p (h d) -> p h d", h=4)), [PB[b1]], [VBs[k]])
                V("act", lambda e, k=k, p1=p1: e.copy(VDs[k][:, :, 0:64], p1[:, 256:512].rearrange("p (h d) -> p h d", h=4)), [PB[b1]], [VDs[k]])
                if CUT < 9:
                    for dst, srcb in ((VB_d, VBs), (VD_d, VDs)):
                        kb.dma("pool", dst[tok0:tok0 + 128], srcb[k][:], reads=[srcb[k]], writes=[])
                    continue
                SK = os.environ.get("P1SK", "")
                for hh in range(4):
                    if "k" in SK:
                        break
                    V("dve", lambda e, k=k, p2=p2, hh=hh: e.tensor_scalar(
                        KFs[k][:, hh * 64:(hh + 1) * 64], p2[:, 256 + hh * 64:256 + (hh + 1) * 64], kdec[:, hh:hh + 1], None, ALU.mult),
                      [PB[b2], kdec], [KFs[k]])
                    V("dve", lambda e, k=k, p2=p2, hh=hh: e.tensor_scalar(
                        KBs[k][:, hh * 64:(hh + 1) * 64], p2[:, 256 + hh * 64:256 + (hh + 1) * 64], kdec[:, 4 + hh:5 + hh], None, ALU.mult),
                      [PB[b2], kdec], [KBs[k]])
                if "a" not in SK:
                  V("act", lambda e, k=k, p3=p3: e.copy(VAs[k][:, :, 0:64], p3.rearrange("p (h d) -> p h d", h=2)), [PB[b3]], [VAs[k]])
                for dst, srcb in ((VB_d, VBs), (VD_d, VDs), (VA_d, VAs), (VC_d, VCs), (KF_d, KFs), (KBk_d, KBs)):
                    if "s" in SK and dst in (VA_d, KF_d, KBk_d):
                        continue
                    kb.dma("pool", dst[tok0:tok0 + 128], srcb[k][:], reads=[srcb[k]], writes=[])

    def attn_generic(st, tag, nheads_q, load_head, qtile_list, chunk_fn, qw, epilogue, nS):
        pass

    def phase2A(l, st, need_ctx):
        QT2 = [kb.sbuf(st, f"aQT{i}", [128, NK], BF16) for i in range(2)]
        KT2 = [kb.sbuf(st, f"aKT{i}", [128, NK], BF16) for i in range(2)]
        for tl in QT2 + KT2:
            V("pool", lambda e, tl=tl: e.memset(tl[64:128, :], 0.0), [], [tl])
        Vv2 = [kb.sbuf(st, f"aV{i}", [128, 34, 128], BF16) for i in range(2)]
        PT = [kb.sbuf(st, f"aPT{i}", [128, 640], BF16) for i in range(3)]
        tq = [kb.sbuf(st, f"atq{i}", [128, 128], F32) for i in range(2)]
        stg = [kb.sbuf(st, f"astg{i}", [64, 512], BF16) for i in range(2)]
        ntile = 34 if need_ctx else 32
        for h in range(4):
            g = h // 2
            QT, KT, Vv = QT2[h % 2], KT2[h % 2], Vv2[h % 2]
            src = fm[f"Aq{h // 2}"]
            kb.dma("sp", QT[0:64, :], src[(h % 2) * 64:(h % 2) * 64 + 64, :], reads=[], writes=[QT])
            kb.dma("sp", KT[0:64, :], fm["Ak"][g * 64:(g + 1) * 64, :], reads=[], writes=[KT])
            kb.dma("sp", Vv[:], VA_d[:, g, :].rearrange("(t p) d -> p t d", p=128), reads=[], writes=[Vv])

            def chunks(i):
                if i >= 32:
                    return [(32, None), (33, None)]
                r = []
                if i > 0:
                    r.append((i - 1, "prev"))
                r.append((i, None))
                if i < 31:
                    r.append((i + 1, "next"))
                return r + [(32, None), (33, None)]

            def s_stage(i):
                ch = chunks(i)
                sb = (i % 3) * 2
                pt = PT[i % 3]

                def f(e):
                    ins = None
                    for ci, (kc, m) in enumerate(ch):
                        o_ = pst[:, sb * 512 + ci * 128: sb * 512 + (ci + 1) * 128]
                        ins = e.matmul(o_, KT[:, kc * 128:(kc + 1) * 128], QT[:, i * 128:(i + 1) * 128], start=True, stop=(m is None))
                        if m is not None:
                            ins = e.matmul(o_, ident, amask[m], start=False, stop=True)
                    return ins
                V("pe", f, [KT, QT, cb], [PB[sb], PB[sb + 1]])
                n = len(ch) * 128
                V("act", lambda e: e.activation(pt[:, :n], pst[:, sb * 512: sb * 512 + n], AF.Exp), [PB[sb], PB[sb + 1]], [pt])

            def o_stage(i):
                ch = chunks(i)
                pt = PT[i % 3]
                ob = 6 + (i % 2)

                def f(e):
                    ins = None
                    for ci, (kc, m) in enumerate(ch):
                        ins = e.matmul(bank(ob, 0, 128, 128), Vv[:, kc, :], pt[:, ci * 128:(ci + 1) * 128], start=(ci == 0), stop=(ci == len(ch) - 1))
                    return ins
                V("pe", f, [Vv, pt], [PB[ob]])
                t_ = tq[i % 2]
                sg = stg[(i // 4) % 2]
                V("act", lambda e: e.activation(t_[64:128, :], bank(ob, 64, 128, 128), AF.Ln, bias=der[64:128, C_ESINK + h:C_ESINK + h + 1], scale=1.0),
                  [PB[ob], der], [t_])
                V("act", lambda e: e.activation(t_[64:128, :], t_[64:128, :], AF.Exp, scale=-1.0), [t_], [t_])
                V("dve", lambda e: e.tensor_tensor(sg[:, (i % 4) * 128:(i % 4 + 1) * 128], bank(ob, 0, 64, 128), t_[64:128, :], ALU.mult),
                  [PB[ob], t_], [sg])
                if i % 4 == 3 or i == ntile - 1:
                    c0 = (i // 4) * 512
                    wd = (i % 4 + 1) * 128
                    kb.dma("pool", attnT[h * 64:(h + 1) * 64, c0:c0 + wd], sg[:, :wd], reads=[sg], writes=[])

            for i in range(ntile + 2):
                if i < ntile:
                    s_stage(i)
                if i >= 2:
                    o_stage(i - 2)

    def phase2D(l, st, need_ctx):
        QT2 = [kb.sbuf(st, f"dQT{i}", [128, NK], BF16) for i in range(2)]
        KT2 = [kb.sbuf(st, f"dKT{i}", [128, NK], BF16) for i in range(2)]
        for tl in QT2 + KT2:
            V("pool", lambda e, tl=tl: e.memset(tl[64:128, :], 0.0), [], [tl])
        Vv2 = [kb.sbuf(st, f"dV{i}", [128, 34, 128], BF16) for i in range(2)]
        tabf2 = [kb.sbuf(st, f"dtabf{i}", [128, 12, 128], F32) for i in range(2)]
        mskf = kb.sbuf(st, "dmskf", [128, 12, 128], F32)
        tab2 = [kb.sbuf(st, f"dtab{i}", [128, 12, 128], BF16) for i in range(2)]
        PT = [kb.sbuf(st, f"dPT{i}", [128, 896], BF16) for i in range(3)]
        tq = [kb.sbuf(st, f"dtq{i}", [64, 128], F32) for i in range(2)]
        stg = [kb.sbuf(st, f"dstg{i}", [64, 512], BF16) for i in range(2)]
        kb.dma("sp", mskf[:], dmask_d[:], reads=[], writes=[mskf])
        ntile = 34 if need_ctx else 32
        for h in range(4):
            QT, KT, Vv, tabf, tab = QT2[h % 2], KT2[h % 2], Vv2[h % 2], tabf2[h % 2], tab2[h % 2]
            src = fm[f"Dq{h // 2}"]
            kb.dma("sp", QT[0:64, :], src[(h % 2) * 64:(h % 2) * 64 + 64, :], reads=[], writes=[QT])
            kb.dma("sp", KT[0:64, :], fm[f"Dk{h // 2}"][(h % 2) * 64:(h % 2) * 64 + 64, :], reads=[], writes=[KT])
            kb.dma("sp", Vv[:], VD_d[:, h, :].rearrange("(t p) d -> p t d", p=128), reads=[], writes=[Vv])
            kb.dma("sp", tabf[:], dbias_d[l, h], reads=[], writes=[tabf])
            V("dve", lambda e: e.tensor_tensor(tab[:], tabf[:], mskf[:], ALU.add), [tabf, mskf], [tab])

            def chunks(a):
                if a >= 32:
                    return [(32, None), (33, None)]
                if 2 <= a <= 29:
                    r = [(a + dl, dl + 2) for dl in range(-2, 3)]
                else:
                    ks = range(0, 4) if a < 2 else range(28, 32)
                    r = [(kc, 5 + (kc - a) + 3) for kc in ks]
                return r + [(32, None), (33, None)]

            def s_stage(i):
                ch = chunks(i)
                sb = (i % 3) * 2
                pt = PT[i % 3]

                def f(e):
                    ins = None
                    for ci, (kc, m) in enumerate(ch):
                        o_ = pst[:, sb * 512 + ci * 128: sb * 512 + (ci + 1) * 128]
                        ins = e.matmul(o_, KT[:, kc * 128:(kc + 1) * 128], QT[:, i * 128:(i + 1) * 128], start=True, stop=(m is None))
                        if m is not None:
                            ins = e.matmul(o_, ident, tab[:, m, :], start=False, stop=True)
                    return ins
                V("pe", f, [KT, QT, cb, tab], [PB[sb], PB[sb + 1]])
                n = len(ch) * 128
                V("act", lambda e: e.activation(pt[:, :n], pst[:, sb * 512: sb * 512 + n], AF.Exp), [PB[sb], PB[sb + 1]], [pt])

            def o_stage(i):
                ch = chunks(i)
                pt = PT[i % 3]
                ob = 6 + (i % 2)

                def f(e):
                    ins = None
                    for ci, (kc, m) in enumerate(ch):
                        ins = e.matmul(bank(ob, 0, 128, 128), Vv[:, kc, :], pt[:, ci * 128:(ci + 1) * 128], start=(ci == 0), stop=(ci == len(ch) - 1))
                    return ins
                V("pe", f, [Vv, pt], [PB[ob]])
                t_ = tq[i % 2]
                sg = stg[(i // 4) % 2]
                V("dve", lambda e: e.reciprocal(t_[:], bank(ob, 64, 128, 128)), [PB[ob]], [t_])
                V("dve", lambda e: e.tensor_tensor(sg[:, (i % 4) * 128:(i % 4 + 1) * 128], bank(ob, 0, 64, 128), t_[:], ALU.mult),
                  [PB[ob], t_], [sg])
                if i % 4 == 3 or i == ntile - 1:
                    c0 = (i // 4) * 512
                    wd = (i % 4 + 1) * 128
                    kb.dma("pool", attnT[768 + h * 64:768 + (h + 1) * 64, c0:c0 + wd], sg[:, :wd], reads=[sg], writes=[])

            for i in range(ntile + 2):
                if i < ntile:
                    s_stage(i)
                if i >= 2:
                    o_stage(i - 2)

    def phase2B(l, st, need_ctx):
        KTz4 = [kb.sbuf(st, f"bKTz{i}", [128, NK], BF16) for i in range(4)]
        KTz = KTz4
        Vv2 = [kb.sbuf(st, f"bV{i}", [128, 34, 128], BF16) for i in range(2)]
        QTs = [kb.sbuf(st, f"bQT{i}", [128, 512], BF16) for i in range(2)]
        for tl in KTz + QTs:
            V("pool", lambda e, tl=tl: e.memset(tl[:], 0.0), [], [tl])
        PT = [kb.sbuf(st, f"bPT{i}", [128, 2, 512], BF16) for i in range(4)]
        Oc = [[kb.sbuf(st, f"bOc{i}{j}", [128, 512], F32) for j in range(2)] for i in range(2)]
        r1 = kb.sbuf(st, "br1", [64, 512], F32)
        t1 = kb.sbuf(st, "bt1", [64, 512], F32)
        t2 = kb.sbuf(st, "bt2", [64, 512], F32)
        osq = kb.sbuf(st, "bosq", [64, 512], BF16)
        sg = [kb.sbuf(st, f"bsg{i}", [64, 512], BF16) for i in range(2)]
        nqb = 9 if need_ctx else 8
        gstep = [0]
        pend_epi = []
        wstg = [kb.sbuf(st, f"wstg{i}", [128, 4096], F32) for i in range(2)]
        wstb = [kb.sbuf(st, f"wstb{i}", [128, 4096], BF16) for i in range(2)]
        jobs = [("wo", i) for i in range(2)] + [("w1", i) for i in range(8)] + [("w2", i) for i in range(8)]
        jn = [0]

        def precast_job():
            if jn[0] >= len(jobs):
                return
            n_ = jn[0]
            jn[0] += 1
            kind, i = jobs[n_]
            a, b = wstg[n_ % 2], wstb[n_ % 2]
            if kind == "wo":
                src = w_out_d[l, :, 4 * i:4 * i + 4, :]
                dst = wo_bf[:, 4 * i:4 * i + 4, :]
                bsrc = b[:].rearrange("p (a b) -> p a b", a=4)
            elif kind == "w1":
                src = w1_d[l, :, i:i + 1, :]
                dst = w1_bf[:, :, i, :].rearrange("f p d -> p f d")
                bsrc = b[:].rearrange("p (f d) -> p f d", f=32)
            else:
                src = w2_d[l, :, 4 * i:4 * i + 4, :]
                dst = None
            kb.dma("sp", a[:].rearrange("p (a b) -> p a b", a=src.shape[1]), src, reads=[], writes=[a])
            V("pool", lambda e: e.tensor_copy(b[:], a[:]), [a], [b])
            if dst is not None:
                kb.dma("pool", dst, bsrc, reads=[b], writes=[])
            else:
                bv = b[:].rearrange("p (f c d) -> p f c d", f=4, c=8)
                for ff in range(4):
                    kb.dma("pool", w2_bf[:, :, 4 * i + ff, :].rearrange("c p d -> p c d"), bv[:, ff], reads=[b], writes=[])
        for h in range(4):
            KTz = KTz4[2 * (h % 2):2 * (h % 2) + 2]
            Vv = Vv2[h % 2]
            kb.dma("sp", KTz[0][0:32, :], fm[f"Bk{h}"][0:32, :], reads=[], writes=[KTz[0]])
            kb.dma("sp", KTz[1][32:64, :], fm[f"Bk{h}"][32:64, :], reads=[], writes=[KTz[1]])
            kb.dma("sp", Vv[:], VB_d[:, h, :].rearrange("(t p) d -> p t d", p=128), reads=[], writes=[Vv])
            for qb in range(nqb):
                Wq = 512 if qb < 8 else 256
                q0 = qb * 512
                itn = h * nqb + qb
                QT = QTs[itn % 2]
                kb.dma("sp", QT[0:64, :Wq], fm[f"Bq{h}"][:, q0:q0 + Wq], reads=[], writes=[QT])
                chs = list(range(34)) if qb < 8 else [32, 33]
                n = len(chs)
                ob = [6, 7]
                slot = {}

                LAGB = 3

                def stage(it):
                    sreads, swrites = [], []
                    do_s = it < n
                    do_o = LAGB <= it < n + LAGB
                    if not do_s and not do_o:
                        return
                    if do_s:
                        kc = chs[it]
                        gs = gstep[0]
                        gstep[0] += 1
                        sb = 2 * (gs % 3)
                        pt = PT[gs % 4]
                        slot[it] = pt
                        sreads += [KTz[0], KTz[1], QT]
                        swrites += [PB[sb], PB[sb + 1]]
                    if do_o:
                        io = it - LAGB
                        kco = chs[io]
                        pto = slot.pop(io)
                        sreads += [Vv, pto]
                        swrites += [PB[ob[0]], PB[ob[1]]]

                    def f(e):
                        ins = None
                        if do_s:
                            e.matmul(bank(sb, 0, 128, Wq), KTz[0][:, kc * 128:(kc + 1) * 128], QT[:, :Wq], start=True, stop=True)
                            ins = e.matmul(bank(sb + 1, 0, 128, Wq), KTz[1][:, kc * 128:(kc + 1) * 128], QT[:, :Wq], start=True, stop=True)
                        if do_o:
                            e.matmul(bank(ob[0], 0, 128, Wq), Vv[:, kco, :], pto[:, 0, :Wq], start=(io == 0), stop=(io == n - 1))
                            ins = e.matmul(bank(ob[1], 0, 128, Wq), Vv[:, kco, :], pto[:, 1, :Wq], start=(io == 0), stop=(io == n - 1))
                        return ins
                    V("pe", f, sreads, swrites)
                    if do_s:
                        src = pst[:, sb * 512:(sb + 2) * 512].rearrange("p (b w) -> p b w", b=2)[:, :, :Wq]
                        V("act", lambda e: e.activation(pt[:, :, :Wq], src, AF.Exp), [PB[sb], PB[sb + 1]], [pt])

                for it in range(n + LAGB):
                    stage(it)
                    if it == min(6, n) and pend_epi:
                        pend_epi.pop(0)()

                oc = Oc[itn % 2]
                V("dve", lambda e: e.tensor_copy(oc[0][:, :Wq], bank(ob[0], 0, 128, Wq)), [PB[ob[0]]], [oc[0]])
                V("dve", lambda e: e.tensor_copy(oc[1][:, :Wq], bank(ob[1], 0, 128, Wq)), [PB[ob[1]]], [oc[1]])

                def epilogue(h=h, qb=qb, Wq=Wq, q0=q0, oc=oc, itn=itn):
                    c1, c2 = oc
                    V("dve", lambda e: e.reciprocal(r1[:, :Wq], c1[64:128, :Wq]), [c1], [r1])
                    V("dve", lambda e: e.tensor_tensor(t1[:, :Wq], c1[0:64, :Wq], r1[:, :Wq], ALU.mult), [c1, r1], [t1])
                    V("dve", lambda e: e.reciprocal(r1[:, :Wq], c2[64:128, :Wq]), [c2], [r1])
                    V("dve", lambda e: e.tensor_tensor(t2[:, :Wq], c2[0:64, :Wq], r1[:, :Wq], ALU.mult), [c2, r1], [t2])
                    V("dve", lambda e: e.scalar_tensor_tensor(t1[:, :Wq], t2[:, :Wq], der[0:64, C_NLAM:C_NLAM + 1], t1[:, :Wq], ALU.mult, ALU.add),
                      [t1, t2, der], [t1])
                    V("act", lambda e: e.activation(osq[:, :Wq], t1[:, :Wq], AF.Square), [t1], [osq])
                    gs = gstep[0]
                    gstep[0] += 1
                    ssb = 2 * (gs % 3)
                    V("pe", lambda e: e.matmul(bank(ssb, 0, 64, Wq), blk64[0:64, 0:64], osq[:, :Wq], start=True, stop=True), [osq, cb], [PB[ssb]])
                    V("act", lambda e: e.activation(t2[:, :Wq], bank(ssb, 0, 64, Wq), AF.Ln, bias=EPS, scale=1.0 / 64), [PB[ssb]], [t2])
                    V("act", lambda e: e.activation(t2[:, :Wq], t2[:, :Wq], AF.Exp, scale=-0.5), [t2], [t2])
                    s_ = sg[itn % 2]
                    V("dve", lambda e: e.scalar_tensor_tensor(s_[:, :Wq], t1[:, :Wq], der[0:64, C_BSUB:C_BSUB + 1], t2[:, :Wq], ALU.mult, ALU.mult),
                      [t1, t2, der], [s_])
                    kb.dma("pool", attnT[256 + h * 64:256 + (h + 1) * 64, q0:q0 + Wq], s_[:, :Wq], reads=[s_], writes=[])
                pend_epi.append(epilogue)
                precast_job()
        while pend_epi:
            pend_epi.pop(0)()
        while jn[0] < len(jobs):
            precast_job()

    def phase2C(l, st, need_ctx):
        Vv = kb.sbuf(st, "cV", [128, 34, 256], BF16)
        KF = kb.sbuf(st, "cKF", [128, 34, 256], BF16)
        KBw = kb.sbuf(st, "cKB", [128, 34, 256], BF16)
        kb.dma("sp", Vv[:], VC_d[:, :].rearrange("(t p) d -> p t d", p=128), reads=[], writes=[Vv])
        kb.dma("sp", KF[:], KF_d[:, :].rearrange("(t p) d -> p t d", p=128), reads=[], writes=[KF])
        kb.dma("sp", KBw[:], KBk_d[:, :].rearrange("(t p) d -> p t d", p=128), reads=[], writes=[KBw])
        S = kb.sbuf(st, "cS", [64, 8, 64], F32)
        Sall = kb.sbuf(st, "cSall", [128, 34, 8, 64], BF16)
        V("pool", lambda e: e.memset(Sall[64:128], 0.0), [], [Sall])
        V("dve", lambda e: e.memset(S[:], 0.0), [], [S])
        order_f = [32, 33] + list(range(32))
        order_b = [33, 32] + list(range(31, -1, -1))
        for s_i in range(34):
            cf, cbk = order_f[s_i], order_b[s_i]
            V("act", lambda e, cf=cf: e.copy(Sall[0:64, cf, 0:4, :], S[:, 0:4, :]), [S], [Sall])
            V("act", lambda e, cbk=cbk: e.copy(Sall[0:64, cbk, 4:8, :], S[:, 4:8, :]), [S], [Sall])
            if s_i == 33:
                break
            bk = s_i % 2

            def f(e, cf=cf, cbk=cbk, bk=bk):
                ins = None
                for h in range(4):
                    ins = e.matmul(pst[0:64, bk * 512 + h * 64: bk * 512 + (h + 1) * 64], KF[:, cf, h * 64:(h + 1) * 64], Vv[:, cf, h * 64:(h + 1) * 64],
                                   start=True, stop=True)
                    ins = e.matmul(pst[0:64, bk * 512 + (4 + h) * 64: bk * 512 + (5 + h) * 64], KBw[:, cbk, h * 64:(h + 1) * 64],
                                   Vv[:, cbk, h * 64:(h + 1) * 64], start=True, stop=True)
                return ins
            V("pe", f, [KF, KBw, Vv], [PB[bk]])
            V("dve", lambda e: e.tensor_tensor(S[:], S[:], G128[:], ALU.mult), [S, G128], [S])
            V("dve", lambda e, bk=bk: e.tensor_tensor(S[:], S[:], bank(bk, 0, 64, 512).rearrange("p (k d) -> p k d", k=8), ALU.add), [S, PB[bk]], [S])
        QT2 = [kb.sbuf(st, f"cQT{i}", [128, NK], BF16) for i in range(2)]
        KT2 = [kb.sbuf(st, f"cKT{i}", [128, NK], BF16) for i in range(2)]
        for tl in QT2 + KT2:
            V("pool", lambda e, tl=tl: e.memset(tl[64:128, :], 0.0), [], [tl])
        GTs = [kb.sbuf(st, f"cGT{i}", [64, 512], BF16) for i in range(2)]
        inF = [kb.sbuf(st, f"cinF{i}", [128, 128], BF16) for i in range(3)]
        inB = [kb.sbuf(st, f"cinB{i}", [128, 128], BF16) for i in range(3)]
        qdF2 = [kb.sbuf(st, f"cqdF{i}", [128, NK], BF16) for i in range(2)]
        qdB2 = [kb.sbuf(st, f"cqdB{i}", [128, NK], BF16) for i in range(2)]
        for tl in qdF2 + qdB2:
            V("pool", lambda e, tl=tl: e.memset(tl[64:128, :], 0.0), [], [tl])
        osq = kb.sbuf(st, "cosq", [64, 512], BF16)
        rr = kb.sbuf(st, "crr", [64, 512], F32)
        tt_ = kb.sbuf(st, "ctt", [64, 512], F32)
        sg = [kb.sbuf(st, f"csg{i}", [64, 512], BF16) for i in range(2)]
        nqb = 9 if need_ctx else 8
        pend_c = []
        for h in range(4):
            QT, KT = QT2[h % 2], KT2[h % 2]
            kb.dma("sp", QT[0:64, :], fm[f"Cq{h // 2}"][(h % 2) * 64:(h % 2) * 64 + 64, :], reads=[], writes=[QT])
            kb.dma("sp", KT[0:64, :], fm[f"Ck{h // 2}"][(h % 2) * 64:(h % 2) * 64 + 64, :], reads=[], writes=[KT])
            qdFh, qdBh = qdF2[h % 2], qdB2[h % 2]
            qv = QT[0:64, :].rearrange("p (c i) -> p c i", i=128)
            V("dve", lambda e: e.tensor_tensor(qdFh[0:64, :].rearrange("p (c i) -> p c i", i=128), qv,
                                               DQ[:, h, :].unsqueeze(1).to_broadcast([64, 34, 128]), ALU.mult), [QT, DQ], [qdFh])
            V("dve", lambda e: e.tensor_tensor(qdBh[0:64, :].rearrange("p (c i) -> p c i", i=128), qv,
                                               DQ[:, 4 + h, :].unsqueeze(1).to_broadcast([64, 34, 128]), ALU.mult), [QT, DQ], [qdBh])
            for qb in range(nqb):
                Wq = 512 if qb < 8 else 256
                q0 = qb * 512
                itn = h * nqb + qb
                GT = GTs[itn % 2]
                kb.dma("sp", GT[:, :Wq], fm[f"Cg{h // 2}"][(h % 2) * 64:(h % 2) * 64 + 64, q0:q0 + Wq], reads=[], writes=[GT])
                ob = 4 + (itn % 2)
                nt = Wq // 128
                def c_stage1(tt):
                    n = qb * 4 + tt
                    r_ = n % 3
                    sbk = r_
                    cs = slice(n * 128, (n + 1) * 128)
                    V("pe", lambda e: e.matmul(bank(sbk, 0, 128, 128), KT[:, cs], QT[:, cs], start=True, stop=True), [KT, QT], [PB[sbk]])
                    V("dve", lambda e: e.tensor_tensor(inF[r_][:], bank(sbk, 0, 128, 128), DT[:, h, :], ALU.mult), [PB[sbk], DT], [inF[r_]])
                    V("dve", lambda e: e.tensor_tensor(inB[r_][:], bank(sbk, 0, 128, 128), DT[:, 4 + h, :], ALU.mult), [PB[sbk], DT], [inB[r_]])

                def c_stage2(tt):
                    n = qb * 4 + tt
                    r_ = n % 3

                    cs = slice(n * 128, (n + 1) * 128)

                    def f(e):
                        o_ = pst[0:64, ob * 512 + tt * 128: ob * 512 + (tt + 1) * 128]
                        e.matmul(o_, Vv[:, n, h * 64:(h + 1) * 64], inF[r_][:], start=True, stop=False)
                        e.matmul(o_, Sall[:, n, h, :], qdFh[:, cs], start=False, stop=False)
                        e.matmul(o_, Vv[:, n, h * 64:(h + 1) * 64], inB[r_][:], start=False, stop=False)
                        return e.matmul(o_, Sall[:, n, 4 + h, :], qdBh[:, cs], start=False, stop=True)
                    V("pe", f, [Vv, Sall, inF[r_], inB[r_], qdFh, qdBh], [PB[ob]])

                for tt in range(nt + 1):
                    if tt < nt:
                        c_stage1(tt)
                    if tt >= 1:
                        c_stage2(tt - 1)
                    if tt == min(2, nt - 1) and pend_c:
                        pend_c.pop(0)()
                def epilogue(h=h, qb=qb, Wq=Wq, q0=q0, ob=ob, GT=GT, itn=itn):
                    po = bank(ob, 0, 64, Wq)
                    V("act", lambda e: e.activation(osq[:, :Wq], po, AF.Square), [PB[ob]], [osq])
                    V("pe", lambda e: e.matmul(bank(7, 0, 64, Wq), blk64[0:64, 0:64], osq[:, :Wq], start=True, stop=True), [osq, cb], [PB[7]])
                    V("act", lambda e: e.activation(rr[:, :Wq], bank(7, 0, 64, Wq), AF.Ln, bias=EPS, scale=1.0 / 64), [PB[7]], [rr])
                    V("act", lambda e: e.activation(rr[:, :Wq], rr[:, :Wq], AF.Exp, scale=-0.5), [rr], [rr])
                    V("dve", lambda e: e.scalar_tensor_tensor(tt_[:, :Wq], po, der[0:64, C_CGN:C_CGN + 1], rr[:, :Wq], ALU.mult, ALU.mult),
                      [PB[ob], der, rr], [tt_])
                    s_ = sg[itn % 2]
                    V("dve", lambda e: e.tensor_tensor(s_[:, :Wq], tt_[:, :Wq], GT[:, :Wq], ALU.mult), [tt_, GT], [s_])
                    kb.dma("pool", attnT[512 + h * 64:512 + (h + 1) * 64, q0:q0 + Wq], s_[:, :Wq], reads=[s_], writes=[])
                pend_c.append(epilogue)
        while pend_c:
            pend_c.pop(0)()

    def phase3(l, st, xsrc, xdst, need_ctx):
        wo = kb.sbuf(st, "wo", [128, 8, D], BF16)
        kb.dma("sp", wo[:], wo_bf[:], reads=[], writes=[wo])
        w1c = [kb.sbuf(st, f"w1c{i}", [128, 8, 128], BF16) for i in range(3)]
        w2c = [kb.sbuf(st, f"w2c{i}", [128, 32, 128], BF16) for i in range(2)]
        xTs = [kb.sbuf(st, f"x3{i}", [128, 8, 512], F32) for i in range(2)]
        aTs = [kb.sbuf(st, f"a3{i}", [128, 8, 512], BF16) for i in range(2)]
        hTs = [kb.sbuf(st, f"h3{i}", [128, 8, 512], BF16) for i in range(2)]
        sq = kb.sbuf(st, "sq3", [128, 8, 512], BF16)
        uT = kb.sbuf(st, "u3", [128, 32, 512], BF16)
        rs = kb.sbuf(st, "rs3", [128, 512], F32)
        tmp = [kb.sbuf(st, f"t3{i}", [128, 512], F32) for i in range(2)]
        xo = [kb.sbuf(st, f"xo{i}", [128, 512], F32) for i in range(2)]
        nblk = 9 if need_ctx else 8
        bc = [0]
        wcnt = [0]

        def nb():
            bc[0] = (bc[0] + 1) % 8
            return bc[0]

        def geom(blk_i):
            return blk_i * 512, (512 if blk_i < 8 else 256), (0 if blk_i < 8 else 1)

        def stageA(blk_i):
            c0, Wb, s = geom(blk_i)
            xT, aT, hT = xTs[blk_i % 2], aTs[blk_i % 2], hTs[blk_i % 2]
            kb.dma("sp", aT[:, :, :Wb], attnT[:, c0:c0 + Wb].rearrange("(c p) t -> p c t", p=128), reads=[], writes=[aT])
            kb.dma("sp", xT[:, :, :Wb], xsrc[:, c0:c0 + Wb].rearrange("(c p) t -> p c t", p=128), reads=[], writes=[xT])
            for c in range(8):
                b_ = nb()

                def f(e, c=c, b_=b_):
                    ins = None
                    for kc in range(8):
                        ins = e.matmul(bank(b_, 0, 128, Wb), wo[:, kc, c * 128:(c + 1) * 128], aT[:, kc, :Wb], start=(kc == 0), stop=(kc == 7))
                    return ins
                V("pe", f, [wo, aT], [PB[b_]])
                V("dve", lambda e, c=c, b_=b_: e.scalar_tensor_tensor(xT[:, c, :Wb], bank(b_, 0, 128, Wb), modT[:, 16 + c, s:s + 1], xT[:, c, :Wb],
                                                                     ALU.mult, ALU.add), [PB[b_], modT, xT], [xT])
                V("act", lambda e, c=c: e.activation(sq[:, c, :Wb], xT[:, c, :Wb], AF.Square), [xT], [sq])
            b0 = nb()

            def ssm(e):
                ins = None
                for c in range(8):
                    ins = e.matmul(bank(b0, 0, 128, Wb), ones_bf, sq[:, c, :Wb], start=(c == 0), stop=(c == 7))
                return ins
            V("pe", ssm, [sq, cb], [PB[b0]])
            V("act", lambda e: e.activation(rs[:, :Wb], bank(b0, 0, 128, Wb), AF.Ln, bias=EPS, scale=1.0 / D), [PB[b0]], [rs])
            V("act", lambda e: e.activation(rs[:, :Wb], rs[:, :Wb], AF.Exp, scale=-0.5), [rs], [rs])
            for c in range(8):
                t_ = tmp[c % 2]
                V("dve", lambda e, c=c, t_=t_: e.scalar_tensor_tensor(t_[:, :Wb], xT[:, c, :Wb], gm[:, 1, c, s:s + 1], rs[:, :Wb], ALU.mult, ALU.mult),
                  [xT, gm, rs], [t_])
                V("act", lambda e, c=c, t_=t_: e.activation(hT[:, c, :Wb], t_[:, :Wb], AF.Identity, bias=modT[:, 24 + c, s:s + 1], scale=1.0),
                  [t_, modT], [hT])

        def stageB(blk_i):
            c0, Wb, s = geom(blk_i)
            xT, hT = xTs[blk_i % 2], hTs[blk_i % 2]
            for fc in range(32):
                wc = w1c[wcnt[0] % 3]
                wcnt[0] += 1
                kb.dma("sp", wc[:], w1_bf[fc], reads=[], writes=[wc])
                b_ = nb()

                def f(e, wc=wc, b_=b_):
                    ins = None
                    for kc in range(8):
                        ins = e.matmul(bank(b_, 0, 128, Wb), wc[:, kc, :], hT[:, kc, :Wb], start=(kc == 0), stop=(kc == 7))
                    return ins
                V("pe", f, [wc, hT], [PB[b_]])
                t_ = tmp[fc % 2]
                V("act", lambda e, b_=b_, t_=t_: e.activation(t_[:, :Wb], bank(b_, 0, 128, Wb), AF.Relu), [PB[b_]], [t_])
                V("dve", lambda e, fc=fc, t_=t_: e.tensor_tensor(uT[:, fc, :Wb], t_[:, :Wb], t_[:, :Wb], ALU.mult), [t_], [uT])
            for c in range(8):
                wc = w2c[c % 2]
                kb.dma("sp", wc[:], w2_bf[c], reads=[], writes=[wc])
                b_ = nb()

                def f(e, wc=wc, b_=b_):
                    ins = None
                    for fc in range(32):
                        ins = e.matmul(bank(b_, 0, 128, Wb), wc[:, fc, :], uT[:, fc, :Wb], start=(fc == 0), stop=(fc == 31))
                    return ins
                V("pe", f, [wc, uT], [PB[b_]])
                o_ = xo[c % 2]
                V("dve", lambda e, c=c, b_=b_, o_=o_: e.scalar_tensor_tensor(o_[:, :Wb], bank(b_, 0, 128, Wb), modT[:, 40 + c, s:s + 1], xT[:, c, :Wb],
                                                                            ALU.mult, ALU.add), [PB[b_], modT, xT], [o_])
                if blk_i < 8 or xdst is xmid:
                    kb.dma("pool", xdst[c * 128:(c + 1) * 128, c0:c0 + Wb], o_[:, :Wb], reads=[o_], writes=[])

        stageA(0)
        for blk_i in range(nblk):
            if blk_i + 1 < nblk:
                stageA(blk_i + 1)
            stageB(blk_i)

    def run_phase(fn, *a):
        with ExitStack() as st:
            fn(*a[:1], st, *a[1:])
            kb.fence()

    for l in range(nlayers):
        need_ctx = l < DEPTH - 1
        xsrc = xin if l == 0 else xmid
        xdst = xmid if l < DEPTH - 1 else outT
        on = lambda nm: only is None or nm in only
        if on("p0"): run_phase(phase0, l)
        if debug and l == 0:
            dbg_mod = kb.dram("dbg_mod", [128, 96 + 64 + 16], F32, "ExternalOutput")
            kb.dma("sp", dbg_mod[:, 0:96], modT[:].rearrange("p a b -> p (a b)"), reads=[modT], writes=[])
            kb.dma("sp", dbg_mod[:, 96:160], der[:], reads=[der], writes=[])
            kb.dma("sp", dbg_mod[:, 160:168], kdec[:], reads=[kdec], writes=[])
            kb.fence()
        if on("p1"): run_phase(phase1, l, xsrc)
        if on("A"): run_phase(phase2A, l, need_ctx)
        if on("B"): run_phase(phase2B, l, need_ctx)
        if on("C"): run_phase(phase2C, l, need_ctx)
        if on("D"): run_phase(phase2D, l, need_ctx)
        if debug and l == 0:
            with ExitStack() as st:
                a = kb.sbuf(st, "dbga", [128, NK], BF16)
                b = kb.sbuf(st, "dbgb", [128, NK], F32)
                for c in range(8):
                    kb.dma("sp", a[:], attnT[c * 128:(c + 1) * 128, :], reads=[], writes=[a])
                    V("dve", lambda e: e.tensor_copy(b[:], a[:]), [a], [b])
                    kb.dma("sp", dbg_attn[c * 128:(c + 1) * 128, :], b[:], reads=[b], writes=[])
                kb.fence()
        if on("p3"): run_phase(phase3, l, xsrc, xdst, need_ctx)
    kb.fence(engines=["sp"])
    top.close()
    return nc, kb


_CACHE = {}


def kernel(**inputs):
    inputs = {k: np.asarray(v) for k, v in inputs.items()}
    maps = _prep(inputs)
    if "nc" not in _CACHE:
        _CACHE["nc"] = build()[0]
    nc = _CACHE["nc"]
    res = run_bass_kernel_spmd(nc, maps, core_ids=list(range(NB)))
    out = np.stack([np.ascontiguousarray(res.results[b]["outT"].T) for b in range(NB)], axis=0)
    return out.astype(np.float32)
```
